# Optimizing a Trainium2 kernel written in Bass

```python
import jax, jax.numpy as jnp
from jax import lax
import numpy as np

D_MODEL = 1024
BATCH = 2
SEQ = 16384
DEPTH = 1

HEAD_DIM = 128
DIL_PAIRS = ((128, 1), (512, 4), (2048, 16))
A_HEADS_PER_GROUP = 2
A_HEADS = A_HEADS_PER_GROUP * len(DIL_PAIRS)
B_Q_HEADS = 4
B_KV_HEADS = 2
B_WINDOW = 128
M_HEADS = 4
N_MEM = 256
D_FF = 2816
ROPE_THETA = 10000.0
BLOCK = 128
EPS = 1e-6
NEG_INF = -1e30

A_WIDTH = A_HEADS * HEAD_DIM
A_OUT = A_HEADS_PER_GROUP * HEAD_DIM
B_WIDTH = B_Q_HEADS * HEAD_DIM
B_KV_WIDTH = B_KV_HEADS * HEAD_DIM
M_WIDTH = M_HEADS * HEAD_DIM
D_IN = 3 * A_WIDTH + B_WIDTH + 2 * B_KV_WIDTH + M_WIDTH
IN_SPLITS = tuple(np.cumsum([A_WIDTH, A_WIDTH, A_WIDTH, B_WIDTH, B_KV_WIDTH, B_KV_WIDTH]).tolist())

kernel_name = "hybrid_dilated_swa_sink_memory_macaron"


def rms_norm(x, g):
    xf = x.astype(jnp.float32)
    y = xf * lax.rsqrt(jnp.mean(xf * xf, axis=-1, keepdims=True) + EPS)
    return (y * g.astype(jnp.float32)).astype(x.dtype)


def swiglu(x, w_in, w_out):
    gate, up = jnp.split(x @ w_in, 2, axis=-1)
    return (jax.nn.silu(gate) * up) @ w_out


def rope(x, pos):
    half = HEAD_DIM // 2
    inv = ROPE_THETA ** (-jnp.arange(half, dtype=jnp.float32) / half)
    ang = pos.astype(jnp.float32)[:, None] * inv[None, :]
    cos = jnp.cos(ang)[None, :, None, :]
    sin = jnp.sin(ang)[None, :, None, :]
    x1 = x[..., :half].astype(jnp.float32)
    x2 = x[..., half:].astype(jnp.float32)
    return jnp.concatenate([x1 * cos - x2 * sin, x2 * cos + x1 * sin], axis=-1).astype(x.dtype)


def banded_attention(q, k, v, max_dist, sink=None):
    b, L, hq, d = q.shape
    hkv = k.shape[2]
    grp = hq // hkv
    blk = min(BLOCK, L)
    nb = -(-L // blk)
    pad = nb * blk - L
    if pad:
        cfg = ((0, 0), (0, pad), (0, 0), (0, 0))
        q, k, v = jnp.pad(q, cfg), jnp.pad(k, cfg), jnp.pad(v, cfg)
    qb = q.reshape(b, nb, blk, hkv, grp, d)
    kb = k.reshape(b, nb, blk, hkv, d)
    vb = v.reshape(b, nb, blk, hkv, d)
    shift = ((0, 0), (1, 0), (0, 0), (0, 0), (0, 0))
    kk = jnp.concatenate([jnp.pad(kb, shift)[:, :-1], kb], axis=2)
    vv = jnp.concatenate([jnp.pad(vb, shift)[:, :-1], vb], axis=2)
    s = jnp.einsum("bnqhgd,bnkhd->bnhgqk", qb, kk).astype(jnp.float32) * (d ** -0.5)
    qpos = jnp.arange(blk)[:, None] + blk
    kpos = jnp.arange(2 * blk)[None, :]
    dist = qpos - kpos
    band = (dist >= 0) & (dist <= max_dist)
    has_prev = (jnp.arange(nb) > 0)[:, None, None] | (kpos >= blk)[None]
    mask = band[None] & has_prev
    s = jnp.where(mask[None, :, None, None], s, NEG_INF)
    m = jnp.max(s, axis=-1)
    if sink is not None:
        sk = sink.astype(jnp.float32).reshape(1, 1, hkv, grp, 1)
        m = jnp.maximum(m, sk)
    p = jnp.exp(s - m[..., None])
    den = jnp.sum(p, axis=-1)
    tot = den + jnp.exp(sk - m) if sink is not None else den
    o = jnp.einsum("bnhgqk,bnkhd->bnqhgd", p.astype(v.dtype), vv).astype(jnp.float32)
    o = o / jnp.moveaxis(tot, -1, 2)[..., None]
    o = o.astype(q.dtype).reshape(b, nb * blk, hq, d)[:, :L]
    lse = jnp.moveaxis(m + jnp.log(den), -1, 2).reshape(b, nb * blk, hq)[:, :L]
    return o, lse


def dilated_group(q, k, v, window, dilation):
    b, s, h, d = q.shape
    L = s // dilation

    def to_sub(t):
        return t.reshape(b, L, dilation, h, d).transpose(0, 2, 1, 3, 4).reshape(b * dilation, L, h, d)

    o, lse = banded_attention(to_sub(q), to_sub(k), to_sub(v), window // dilation)
    o = o.reshape(b, dilation, L, h, d).transpose(0, 2, 1, 3, 4).reshape(b, s, h, d)
    lse = lse.reshape(b, dilation, L, h).transpose(0, 2, 1, 3).reshape(b, s, h)
    return o, lse


def memory_attention(q, mem_n, w_mem_kv):
    b, n, _ = mem_n.shape
    mk, mv = jnp.split(mem_n @ w_mem_kv, 2, axis=-1)
    mk = mk.reshape(b, n, M_HEADS, HEAD_DIM)
    mv = mv.reshape(b, n, M_HEADS, HEAD_DIM)
    s = jnp.einsum("bshd,bmhd->bhsm", q, mk).astype(jnp.float32) * (HEAD_DIM ** -0.5)
    p = jax.nn.softmax(s, axis=-1)
    return jnp.einsum("bhsm,bmhd->bshd", p.astype(mv.dtype), mv)


def setup_inputs(seed: int = 0) -> dict:
    key = jax.random.key(seed)
    ks = jax.random.split(key, 24)
    f = jnp.float32

    def w(k, shape, fan_in):
        return jax.random.normal(k, (DEPTH,) + shape, f) * fan_in ** -0.5

    def gain(k):
        return 1.0 + 0.1 * jax.random.normal(k, (DEPTH, D_MODEL), f)

    return {
        "x": jax.random.normal(ks[0], (BATCH, SEQ, D_MODEL), f),
        "mem": jax.random.normal(ks[1], (BATCH, N_MEM, D_MODEL), f),
        "ffn1_norm_pre": gain(ks[2]),
        "ffn1_w_in": w(ks[3], (D_MODEL, 2 * D_FF), D_MODEL),
        "ffn1_w_out": w(ks[4], (D_FF, D_MODEL), D_FF),
        "ffn1_norm_post": gain(ks[5]),
        "mix_norm_pre": gain(ks[6]),
        "w_in": w(ks[7], (D_MODEL, D_IN), D_MODEL),
        "sinks": 0.5 * jax.random.normal(ks[8], (DEPTH, B_Q_HEADS), f),
        "mem_norm": gain(ks[9]),
        "w_mem_kv": w(ks[10], (D_MODEL, 2 * M_WIDTH), D_MODEL),
        "w_gate": w(ks[11], (D_MODEL, 3 * D_MODEL), D_MODEL),
        "b_gate": 0.01 * jax.random.normal(ks[12], (DEPTH, 3 * D_MODEL), f),
        "w_o_a": w(ks[13], (A_OUT, D_MODEL), A_OUT),
        "w_o_b": w(ks[14], (B_WIDTH, D_MODEL), B_WIDTH),
        "w_o_m": w(ks[15], (M_WIDTH, D_MODEL), M_WIDTH),
        "w_out": w(ks[16], (D_MODEL, D_MODEL), D_MODEL),
        "mix_norm_post": gain(ks[17]),
        "ffn2_norm_pre": gain(ks[18]),
        "ffn2_w_in": w(ks[19], (D_MODEL, 2 * D_FF), D_MODEL),
        "ffn2_w_out": w(ks[20], (D_FF, D_MODEL), D_FF),
        "ffn2_norm_post": gain(ks[21]),
    }


def reference(x, mem, ffn1_norm_pre, ffn1_w_in, ffn1_w_out, ffn1_norm_post, mix_norm_pre,
              w_in, sinks, mem_norm, w_mem_kv, w_gate, b_gate, w_o_a, w_o_b, w_o_m, w_out,
              mix_norm_post, ffn2_norm_pre, ffn2_w_in, ffn2_w_out, ffn2_norm_post):
    b, s, _ = x.shape
    pos = jnp.arange(s)
    h = x
    for l in range(DEPTH):
        f1 = swiglu(rms_norm(h, ffn1_norm_pre[l]), ffn1_w_in[l], ffn1_w_out[l])
        h = h + 0.5 * rms_norm(f1, ffn1_norm_post[l])

        u = rms_norm(h, mix_norm_pre[l])
        aq, ak, av, bq, bk, bv, mq = jnp.split(u @ w_in[l], IN_SPLITS, axis=-1)

        aq = rope(aq.reshape(b, s, A_HEADS, HEAD_DIM), pos).reshape(b, s, len(DIL_PAIRS), A_HEADS_PER_GROUP, HEAD_DIM)
        ak = rope(ak.reshape(b, s, A_HEADS, HEAD_DIM), pos).reshape(b, s, len(DIL_PAIRS), A_HEADS_PER_GROUP, HEAD_DIM)
        av = av.reshape(b, s, len(DIL_PAIRS), A_HEADS_PER_GROUP, HEAD_DIM)
        outs, lses = [], []
        for g, (window, dilation) in enumerate(DIL_PAIRS):
            o_g, lse_g = dilated_group(aq[:, :, g], ak[:, :, g], av[:, :, g], window, dilation)
            outs.append(o_g)
            lses.append(lse_g)
        wts = jax.nn.softmax(jnp.stack(lses, axis=0), axis=0)
        o_a = jnp.sum(wts[..., None] * jnp.stack(outs, axis=0).astype(jnp.float32), axis=0)
        o_a = o_a.astype(x.dtype).reshape(b, s, A_OUT)

        bq = rope(bq.reshape(b, s, B_Q_HEADS, HEAD_DIM), pos)
        bk = rope(bk.reshape(b, s, B_KV_HEADS, HEAD_DIM), pos)
        bv = bv.reshape(b, s, B_KV_HEADS, HEAD_DIM)
        o_b, _ = banded_attention(bq, bk, bv, B_WINDOW - 1, sink=sinks[l])
        o_b = o_b.reshape(b, s, B_WIDTH)

        o_m = memory_attention(mq.reshape(b, s, M_HEADS, HEAD_DIM), rms_norm(mem, mem_norm[l]), w_mem_kv[l])
        o_m = o_m.reshape(b, s, M_WIDTH)

        g_a, g_b, g_m = jnp.split(jax.nn.sigmoid(u @ w_gate[l] + b_gate[l]), 3, axis=-1)
        merged = g_a * (o_a @ w_o_a[l]) + g_b * (o_b @ w_o_b[l]) + g_m * (o_m @ w_o_m[l])
        h = h + rms_norm(merged @ w_out[l], mix_norm_post[l])

        f2 = swiglu(rms_norm(h, ffn2_norm_pre[l]), ffn2_w_in[l], ffn2_w_out[l])
        h = h + 0.5 * rms_norm(f2, ffn2_norm_post[l])
    return h
```

```python
import contextlib
import numpy as np
import concourse.bass as bass
import concourse.mybir as mybir
from concourse.bass_utils import run_bass_kernel_spmd

F32 = mybir.dt.float32
BF16 = mybir.dt.bfloat16
ALU = mybir.AluOpType
AF = mybir.ActivationFunctionType

D = 1024
DFF = 2816
NF = 22
TT = 512
NHALO = 4
NOWN = 8
NTILE = NHALO + NOWN
NTOK = NTILE * TT
EPS = 1e-6
SCALE = 128 ** -0.5
SLOT = 6144
NSLOT = 4
ROPE_ADD_ENG = "pool"

M_A1, M_B, M_G2OWN, M_G2PREV, M_G3OWN, M_G3ALL, M_G3PREV = 0, 256, 512, 1024, 1536, 2048, 2560
NMASK = 3072


class Op:
    __slots__ = ("eng", "emit", "deps", "token", "needs_token", "is_dma", "dma_key", "epoch", "idx")


class Prog:
    ENGS = ("pe", "act", "dve", "pool", "sp")

    def __init__(self):
        self.ops = []
        self.lastw = {}
        self.readers = {}
        self.epoch = 0

    def new_epoch(self):
        self.epoch += 1

    def seed(self, new_keys, old_keys):
        ops = {}
        for o in old_keys:
            w = self.lastw.get(o)
            if w is not None:
                ops[w.idx] = w
            for d in self.readers.get(o, {}).values():
                ops[d.idx] = d
        for k in new_keys:
            rd = self.readers.setdefault(k, {})
            for idx, op in ops.items():
                rd[("seed", idx)] = op

    def add(self, eng, emit, reads=(), writes=(), dma_key=None, after=()):
        op = Op()
        op.eng = eng
        op.emit = emit
        op.is_dma = dma_key is not None
        op.dma_key = dma_key
        op.needs_token = op.is_dma
        op.token = None
        op.epoch = self.epoch
        op.idx = len(self.ops)
        deps = {}
        for r in reads:
            w = self.lastw.get(r)
            if w is not None:
                deps[w.idx] = w
        for k in writes:
            w = self.lastw.get(k)
            if w is not None:
                deps[w.idx] = w
            rd = self.readers.get(k)
            if rd:
                for d in rd.values():
                    deps[d.idx] = d
        for d in after:
            deps[d.idx] = d
        op.deps = []
        for d in deps.values():
            if d.eng == "pe" and eng == "pe" and not d.is_dma and not op.is_dma:
                continue
            d.needs_token = True
            op.deps.append(d)
        for r in reads:
            rd = self.readers.setdefault(r, {})
            rd[("dma", op.idx) if op.is_dma else eng] = op
        for k in writes:
            self.lastw[k] = op
            self.readers[k] = {}
        self.ops.append(op)
        return op

    def emit_all(self, nc):
        with contextlib.ExitStack() as st:
            sems = {}

            def getsem(key):
                if key not in sems:
                    sems[key] = st.enter_context(nc.semaphore("s%d" % len(sems)))
                return sems[key]

            counts = {}
            for op in self.ops:
                if op.is_dma:
                    key = ("dma", op.dma_key)
                    counts[key] = counts.get(key, 0) + 16
                    op.token = (getsem(key), counts[key])
                elif op.needs_token:
                    key = (op.eng, op.epoch)
                    counts[key] = counts.get(key, 0) + 1
                    op.token = (getsem(key), counts[key])
            self.n_sems = len(sems)
            self.sem_counts = counts
            block = st.enter_context(nc.Block())
            per = {e: [o for o in self.ops if o.eng == e] for e in self.ENGS}

            def run(e, ops):
                known = {}
                for op in ops:
                    waits = {}
                    for d in op.deps:
                        sem, val = d.token
                        sid = id(sem)
                        if known.get(sid, 0) < val:
                            if sid not in waits or waits[sid][1] < val:
                                waits[sid] = (sem, val)
                    for sid, (sem, val) in waits.items():
                        e.wait_ge(sem, val)
                        known[sid] = val
                    if op.emit is None:
                        continue
                    ins = op.emit(e)
                    if op.token is not None:
                        ins.then_inc(op.token[0], 16 if op.is_dma else 1)

            @block.tensor
            def _(e):
                run(e, per["pe"])

            @block.scalar
            def _(e):
                run(e, per["act"])

            @block.vector
            def _(e):
                run(e, per["dve"])

            @block.gpsimd
            def _(e):
                run(e, per["pool"])

            @block.sync
            def _(e):
                run(e, per["sp"])


def piece_table():
    t = {}
    for w in (1, 2):
        for j in range(11):
            t["f%d_in%d" % (w, j)] = [("cols", "ffn%d_w_in" % w, 256 * j, 256, 0),
                                      ("cols", "ffn%d_w_in" % w, DFF + 256 * j, 256, 2048)]
        for m in range(4):
            t["f%d_out%d" % (w, m)] = [("ocols", "ffn%d_w_out" % w, 256 * m, 256, 0)]
    t["aq0"] = [("cols", "w_in", 0, 512, 0)]
    t["aq1"] = [("cols", "w_in", 512, 256, 0)]
    t["ak0"] = [("cols", "w_in", 768, 512, 0)]
    t["ak1"] = [("cols", "w_in", 1280, 256, 0)]
    t["av0"] = [("cols", "w_in", 1536, 512, 0)]
    t["av1"] = [("cols", "w_in", 2048, 256, 0)]
    t["bq"] = [("cols", "w_in", 2304, 512, 0)]
    t["bkv"] = [("cols", "w_in", 2816, 512, 0)]
    t["mq"] = [("cols", "w_in", 3328, 512, 0)]
    for br, src in enumerate(("w_o_a", "w_o_b", "w_o_m")):
        for half in range(2):
            t["mg%d%d" % (br, half)] = [("cols", "w_gate", 1024 * br + 512 * half, 512, 0),
                                        ("rcols", src, 512 * half, 512, 4096)]
    t["wo0"] = [("cols", "w_out", 0, 512, 0)]
    t["wo1"] = [("cols", "w_out", 512, 512, 0)]
    return t


def part_elems(part):
    kind, _, _, n, _ = part
    if kind == "cols":
        return 8 * n
    if kind == "ocols":
        return NF * n
    return (WSHAPES[part[1]][0] // 128) * n


def piece_used(parts):
    return max(p[4] + part_elems(p) for p in parts)


def tile_pieces(i, stage=0):
    ffn1 = ["f1_in%d" % j for j in range(11)] + ["f1_out%d" % m for m in range(4)]
    if i < NHALO - 1:
        return ffn1 + ["ak1", "av1"]
    if i == NHALO - 1:
        return ffn1 + ["ak0", "ak1", "av0", "av1", "bkv"]
    if stage == 1:
        return ffn1
    mix = ["aq0", "aq1", "ak0", "ak1", "av0", "av1", "bq", "bkv", "mq",
           "mg00", "mg01", "mg10", "mg11", "mg20", "mg21", "wo0", "wo1"]
    ffn2 = ["f2_in%d" % j for j in range(11)] + ["f2_out%d" % m for m in range(4)]
    if stage in (21, 22):
        return ffn1 + mix[:6]
    if stage == 23:
        return ffn1 + mix[:8]
    if stage == 24:
        return ffn1 + mix[:9]
    if stage == 2:
        return ffn1 + mix
    return ffn1 + mix + ffn2


WNAMES = ["ffn1_w_in", "ffn1_w_out", "w_in", "w_mem_kv", "w_gate", "w_o_a", "w_o_b", "w_o_m", "w_out",
          "ffn2_w_in", "ffn2_w_out"]
WSHAPES = {"ffn1_w_in": [D, 2 * DFF], "ffn1_w_out": [DFF, D], "w_in": [D, 3840], "w_mem_kv": [D, D],
           "w_gate": [D, 3 * D], "w_o_a": [256, D], "w_o_b": [512, D], "w_o_m": [512, D], "w_out": [D, D],
           "ffn2_w_in": [D, 2 * DFF], "ffn2_w_out": [DFF, D]}


def build_program(first_tile=0, last_tile=NTILE, stage=0):
    nc = bass.Bass("TRN2", target_bir_lowering=False)
    dr = {}
    for name in WNAMES:
        dr[name] = nc.dram_tensor(name, WSHAPES[name], F32, kind="ExternalInput").ap()
    xT = nc.dram_tensor("xT", [D, NTOK], F32, kind="ExternalInput").ap()
    csd = nc.dram_tensor("cs", [128, 2, NTOK], F32, kind="ExternalInput").ap()
    hbd = nc.dram_tensor("hb", [128, 1], F32, kind="ExternalInput").ap()
    memT = nc.dram_tensor("memT", [D, 256], F32, kind="ExternalInput").ap()
    masksd = nc.dram_tensor("masks", [128, NMASK], F32, kind="ExternalInput").ap()
    rpermd = nc.dram_tensor("rperm", [128, 128], F32, kind="ExternalInput").ap()
    gainsd = nc.dram_tensor("gains", [128, 7, 8], F32, kind="ExternalInput").ap()
    bgated = nc.dram_tensor("bgate", [128, 24], F32, kind="ExternalInput").ap()
    sinksd = nc.dram_tensor("sinks", [1, 4], F32, kind="ExternalInput").ap()
    yT = nc.dram_tensor("yT", [D, NOWN * TT], F32, kind="ExternalOutput").ap()

    ptab = piece_table()
    pnames = list(ptab.keys())
    pid = {n: k for k, n in enumerate(pnames)}
    wsc = nc.dram_tensor("wsc", [len(pnames), 128, SLOT], BF16).ap()

    P = Prog()
    st = contextlib.ExitStack()
    with st:
        def sb(name, shape, dt):
            return st.enter_context(nc.sbuf_tensor(name, shape, dt))

        ones_f = sb("ones_f", [128, 128], F32)
        ones_b = sb("ones_b", [128, 128], BF16)
        zeros_b = sb("zeros_b", [128, 128], BF16)
        rperm = sb("rperm_sb", [128, 128], BF16)
        masks = sb("masks_sb", [128, NMASK], BF16)
        gains = sb("gains_sb", [128, 7, 8], F32)
        ghalf = sb("ghalf_sb", [128, 2, 8], F32)
        bgate = sb("bgate_sb", [128, 24], F32)
        esink = sb("esink_sb", [128, 4], F32)
        hb = sb("hb_sb", [128, 1], F32)
        hbuf = sb("hbuf", [128, 8, TT], F32)
        xn = sb("xn", [128, 8, TT], BF16)
        sq8 = sb("sq8", [128, 8, TT], BF16)
        ones_s = sb("ones_s", [128, 128], BF16)
        lnm = sb("lnm", [128, TT], F32)
        rstd = sb("rstd", [128, TT], F32)
        fbuf = sb("fbuf", [128, 8, TT], F32)
        mkT = sb("mkT", [128, 4, 256], BF16)
        mv = sb("mv", [128, 2, 512], BF16)
        NR = {0: 2, 1: 2, 2: 5, 3: 2}
        Kr = {g: sb("K%d" % g, [128, NR[g], 2, TT], BF16) for g in range(4)}
        Vr = {g: sb("V%d" % g, [128, NR[g], 4, 256], BF16) for g in range(4)}
        wslot = [sb("wslot%d" % s, [128, SLOT], BF16) for s in range(NSLOT)]
        scr = sb("scr", [128, 24576], BF16)
        ps = [st.enter_context(nc.psum_tensor("ps%d" % k, [128, TT], F32)) for k in range(8)]

        def f32view(lo_b, hi_b):
            return scr[:, lo_b // 2:hi_b // 2].bitcast(F32)

        hid = scr[:, 0:NF * TT].rearrange("p (c t) -> p c t", t=TT)
        xin = f32view(22528, 38912).rearrange("p (c t) -> p c t", t=TT)
        sg = f32view(38912, 43008).rearrange("p (c t) -> p c t", t=TT)
        qT = scr[:, 0:3072].rearrange("p (c t) -> p c t", t=TT)
        xb = scr[:, 3072:4096].rearrange("p (c t) -> p c t", t=TT)
        rt = f32view(8192, 16384).rearrange("p (c t) -> p c t", t=TT)
        Pb = scr[:, 10240:12288].rearrange("p (c t) -> p c t", t=TT)
        oT = scr[:, 12288:17408].rearrange("p (c t) -> p c t", t=TT)
        rec = f32view(34816, 36864)
        sig = f32view(36864, 40960).rearrange("p (c t) -> p c t", t=TT)
        prod = f32view(40960, 43008)
        merged = scr[:, 0:4096].rearrange("p (c t) -> p c t", t=TT)
        cst = f32view(43008, 47104).rearrange("p (c t) -> p c t", t=TT)

        FFN_KEYS = [("hid", f) for f in range(NF)] + ["xin", ("sg", 0), ("sg", 1)]
        MIX_KEYS = ["qT0", "qT1", "qT2", "qT3", "qT4", "qT5", ("xb", 0), ("xb", 1)] + [("rt", k) for k in range(4)] + \
                   [("Pb", k) for k in range(4)] + [("oT", k) for k in range(10)] + ["rec", ("sig", 0), ("sig", 1), "prod"] + \
                   [("merged", k) for k in range(8)]

        bank_ctr = [0]

        def next_bank(pool=(0, 1, 2, 3, 4, 5, 6, 7)):
            b = pool[bank_ctr[0] % len(pool)]
            bank_ctr[0] += 1
            return b

        def bk(b):
            return ("ps", b)

        P.add("sp", lambda e: e.dma_start(out=gains[:], in_=gainsd), writes=["gains"], dma_key="c_gains")
        P.add("sp", lambda e: e.dma_start(out=bgate[:], in_=bgated), writes=["bgate"], dma_key="c_bgate")
        P.add("sp", lambda e: e.dma_start(out=hb[:], in_=hbd), writes=["hb"], dma_key="c_hb")
        P.add("sp", lambda e: e.dma_start(out=esink[:], in_=sinksd.partition_broadcast(128)), writes=["esink"], dma_key="c_sink")
        for hh_ in range(2):
            P.add("pool", lambda e, hh_=hh_: e.dma_start(out=masks[:, hh_ * 1536:(hh_ + 1) * 1536], in_=masksd[:, hh_ * 1536:(hh_ + 1) * 1536]),
                  writes=["masks"], dma_key="c_masks")
        for g_ in range(4):
            P.add("pool", lambda e, g_=g_: e.memset(Kr[g_][:], 0.0), writes=[("K", g_, s_, h_) for s_ in range(NR[g_]) for h_ in range(2)])
            P.add("pool", lambda e, g_=g_: e.memset(Vr[g_][:], 0.0), writes=[("V", g_, s_, t_) for s_ in range(NR[g_]) for t_ in range(4)])
        P.add("pool", lambda e: e.dma_start(out=rperm[:], in_=rpermd), writes=["rperm"], dma_key="c_rperm")
        P.add("dve", lambda e: e.memset(ones_s[:], 1.0 / D), writes=["ones_s"])
        P.add("dve", lambda e: e.memset(ones_b[:], 1.0), writes=["ones_b"])
        P.add("dve", lambda e: e.memset(zeros_b[:], 0.0), writes=["zeros_b"])
        P.add("act", lambda e: e.activation(out=esink[:], in_=esink[:], func=AF.Exp), reads=["esink"], writes=["esink"])
        P.add("dve", lambda e: e.tensor_scalar(out=ghalf[:, 0, :], in0=gains[:, 1, :], scalar1=0.5, scalar2=None, op0=ALU.mult),
              reads=["gains"], writes=["ghalf"])
        P.add("dve", lambda e: e.tensor_scalar(out=ghalf[:, 1, :], in0=gains[:, 5, :], scalar1=0.5, scalar2=None, op0=ALU.mult),
              reads=["gains"], writes=["ghalf"])

        def load_x(i):
            P.seed(["xin"], MIX_KEYS)
            P.add("sp", lambda e: e.dma_start(out=xin, in_=xT[:, i * TT:(i + 1) * TT].rearrange("(c p) t -> p c t", p=128)),
                  writes=["xin"], dma_key="xin")

        def load_cs(i):
            P.add("sp", lambda e: e.dma_start(out=cst, in_=csd[:, :, i * TT:(i + 1) * TT]), writes=["cst"], dma_key="cst")

        def store_y(i):
            n = i - NHALO
            P.add("sp", lambda e: e.dma_start(out=yT[:, n * TT:(n + 1) * TT].rearrange("(c p) t -> p c t", p=128), in_=hbuf[:]),
                  reads=[("h", c) for c in range(8)], writes=[("y", i)], dma_key="ystore")

        tiles = list(range(first_tile, last_tile))
        seq = []
        for i in tiles:
            for nm in tile_pieces(i, stage):
                seq.append(nm)
        used_pieces = []
        for nm in seq:
            if nm not in used_pieces:
                used_pieces.append(nm)
        def cast_piece(nm, dst_img, key_prefix, dma_key):
            for k, part in enumerate(ptab[nm]):
                kind, src, c0, n, off = part
                W = dr[src]
                ne = part_elems(part)
                if kind == "cols":
                    o = dst_img[:, off:off + ne].rearrange("p (c n) -> p c n", n=n)
                    s_ = W[:, c0:c0 + n].rearrange("(c p) n -> p c n", p=128)
                elif kind == "ocols":
                    o = dst_img[:, off:off + ne].rearrange("p (c n) -> p c n", n=n)
                    s_ = W[:, c0:c0 + n].rearrange("(c p) n -> p c n", p=128)
                else:
                    o = dst_img[:, off:off + ne].rearrange("p (c n) -> p c n", n=n)
                    s_ = W[:, c0:c0 + n].rearrange("(c p) n -> p c n", p=128)
                thr = []
                prevg = ("wscg", dma_key[1] - 2)
                if prevg in last_cast:
                    thr = [last_cast[prevg]]
                last_cast[dma_key] = P.add("pool", lambda e, o=o, s_=s_: e.dma_start(out=o, in_=s_), writes=[(key_prefix, k)], dma_key=dma_key, after=thr)

        last_cast = {}
        piece_group = {}
        wstate = {"next_load": 0, "next_use": 0}

        def issue_loads(upto):
            while wstate["next_load"] <= upto and wstate["next_load"] < len(seq):
                k = wstate["next_load"]
                nm = seq[k]
                s = k % NSLOT
                used = piece_used(ptab[nm])
                src = wsc[pid[nm]]
                P.add("sp", lambda e, s=s, used=used, src=src: e.dma_start(out=wslot[s][:, 0:used], in_=src[:, 0:used]),
                      reads=[(("wsc", nm), kk) for kk in range(len(ptab[nm]))], writes=[("wslot", s)], dma_key=("wslot", s),
                      after=[last_cast[piece_group[nm]]])
                wstate["next_load"] += 1

        def next_piece(expect):
            k = wstate["next_use"]
            assert seq[k] == expect, (seq[k], expect)
            issue_loads(k + NSLOT - 1)
            wstate["next_use"] += 1
            s = k % NSLOT
            return wslot[s], ("wslot", s)

        def stats_square(c, src_ap, src_key_list, ncols=TT):
            P.add("act", lambda e: e.activation(out=sq8[:, c, 0:ncols], in_=src_ap, func=AF.Square),
                  reads=src_key_list, writes=[("sq8", c)])

        def stats_finish(ncols=TT):
            b = next_bank()
            for c in range(8):
                P.add("pe", lambda e, c=c: e.matmul(ps[b][:, 0:ncols], lhsT=ones_s[:], rhs=sq8[:, c, 0:ncols], start=(c == 0), stop=(c == 7)),
                      reads=["ones_s", ("sq8", c)], writes=[bk(b)])
            P.add("act", lambda e: e.activation(out=lnm[:, 0:ncols], in_=ps[b][:, 0:ncols], func=AF.Ln, bias=EPS, scale=1.0),
                  reads=[bk(b)], writes=["lnm"])
            P.add("act", lambda e: e.activation(out=rstd[:, 0:ncols], in_=lnm[:, 0:ncols], func=AF.Exp, scale=-0.5),
                  reads=["lnm"], writes=["rstd"])

        def rms_stats(src_chunks, src_keys, ncols=TT):
            for c in range(8):
                stats_square(c, src_chunks[c], [src_keys[c]], ncols)
            stats_finish(ncols)

        def normalize(src_chunks, src_keys, gidx, ncols=TT):
            for c in range(8):
                P.add("dve", lambda e, c=c: e.scalar_tensor_tensor(out=xn[:, c, 0:ncols], in0=src_chunks[c], scalar=gains[:, gidx, c:c + 1],
                                                                   in1=rstd[:, 0:ncols], op0=ALU.mult, op1=ALU.mult),
                      reads=[src_keys[c], "gains", "rstd"], writes=[("xn", c)])

        def ffn(which, src, src_keys, resid, resid_keys):
            src_chunks = [src[:, c, :] for c in range(8)]
            P.seed(FFN_KEYS[:NF] + FFN_KEYS[NF + 1:], MIX_KEYS)
            rms_stats(src_chunks, src_keys)
            normalize(src_chunks, src_keys, 0 if which == 1 else 4)
            for j in range(11):
                wt, wkey = next_piece("f%d_in%d" % (which, j))
                wg = wt[:, 0:2048].rearrange("p (c n) -> p c n", n=256)
                wu = wt[:, 2048:4096].rearrange("p (c n) -> p c n", n=256)
                for half in range(2):
                    f = 2 * j + half
                    bg, bu = next_bank(), next_bank()
                    for c in range(8):
                        P.add("pe", lambda e, c=c, bg=bg, wg=wg, half=half: e.matmul(ps[bg][:], lhsT=wg[:, c, half * 128:(half + 1) * 128], rhs=xn[:, c, :],
                                                                                     start=(c == 0), stop=(c == 7)),
                              reads=[wkey, ("xn", c)], writes=[bk(bg)])
                    for c in range(8):
                        P.add("pe", lambda e, c=c, bu=bu, wu=wu, half=half: e.matmul(ps[bu][:], lhsT=wu[:, c, half * 128:(half + 1) * 128], rhs=xn[:, c, :],
                                                                                     start=(c == 0), stop=(c == 7)),
                              reads=[wkey, ("xn", c)], writes=[bk(bu)])
                    P.add("act", lambda e, f=f, bg=bg: e.activation(out=sg[:, f % 2, :], in_=ps[bg][:], func=AF.Silu),
                          reads=[bk(bg)], writes=[("sg", f % 2)])
                    P.add("dve", lambda e, f=f, bu=bu: e.tensor_tensor(out=hid[:, f, :], in0=sg[:, f % 2, :], in1=ps[bu][:], op=ALU.mult),
                          reads=[("sg", f % 2), bk(bu)], writes=[("hid", f)])
            for m in range(4):
                wt, wkey = next_piece("f%d_out%d" % (which, m))
                wo = wt[:, 0:NF * 256].rearrange("p (c n) -> p c n", n=256)
                for oo in range(2):
                    o = 2 * m + oo
                    b = next_bank()
                    for f in range(NF):
                        P.add("pe", lambda e, f=f, b=b, wo=wo, oo=oo: e.matmul(ps[b][:], lhsT=wo[:, f, oo * 128:(oo + 1) * 128], rhs=hid[:, f, :],
                                                                               start=(f == 0), stop=(f == NF - 1)),
                              reads=[wkey, ("hid", f)], writes=[bk(b)])
                    stats_square(o, ps[b][:], [bk(b)])
                    P.add("dve", lambda e, o=o, b=b: e.tensor_copy(out=fbuf[:, o, :], in_=ps[b][:]), reads=[bk(b), ("sq8", o)], writes=[("f", o)])
            stats_finish()
            gi = 0 if which == 1 else 1
            for c in range(8):
                P.add("dve", lambda e, c=c: e.scalar_tensor_tensor(out=fbuf[:, c, :], in0=fbuf[:, c, :], scalar=ghalf[:, gi, c:c + 1],
                                                                   in1=rstd[:], op0=ALU.mult, op1=ALU.mult),
                      reads=[("f", c), "ghalf", "rstd"], writes=[("f", c)])
                P.add("dve", lambda e, c=c: e.tensor_tensor(out=hbuf[:, c, :], in0=fbuf[:, c, :], in1=resid[:, c, :], op=ALU.add),
                      reads=[("f", c), resid_keys[c]], writes=[("h", c)])

        def proj_fm(wt, wkey, ncols, col0, rope, dst_fn):
            wv = wt[:, 0:8 * ncols].rearrange("p (c n) -> p c n", n=ncols)
            b = next_bank()
            for c in range(8):
                P.add("pe", lambda e, c=c: e.matmul(ps[b][:], lhsT=wv[:, c, col0:col0 + 128], rhs=xn[:, c, :], start=(c == 0), stop=(c == 7)),
                      reads=[wkey, ("xn", c)], writes=[bk(b)])
            dst, dkeys = dst_fn()
            if not rope:
                P.add("act", lambda e: e.copy(out=dst, in_=ps[b][:]), reads=[bk(b)], writes=dkeys)
                return
            k = rope_ctr[0] % 2
            rope_ctr[0] += 1
            P.add("act", lambda e: e.copy(out=xb[:, k, :], in_=ps[b][:]), reads=[bk(b)], writes=[("xb", k)])
            b2 = next_bank()
            P.add("pe", lambda e: e.matmul(ps[b2][:], lhsT=rperm[:], rhs=xb[:, k, :], start=True, stop=True),
                  reads=["rperm", ("xb", k)], writes=[bk(b2)])
            P.add("dve", lambda e: e.tensor_tensor(out=rt[:, 2 * k, :], in0=ps[b][:], in1=cst[:, 0, :], op=ALU.mult),
                  reads=[bk(b), "cst", ("xb", k)], writes=[("rt", 2 * k)])
            P.add("dve", lambda e: e.tensor_tensor(out=rt[:, 2 * k + 1, :], in0=ps[b2][:], in1=cst[:, 1, :], op=ALU.mult),
                  reads=[bk(b2), "cst"], writes=[("rt", 2 * k + 1)])
            P.add(ROPE_ADD_ENG, lambda e: e.tensor_tensor(out=dst, in0=rt[:, 2 * k, :], in1=rt[:, 2 * k + 1, :], op=ALU.add),
                  reads=[("rt", 2 * k), ("rt", 2 * k + 1)], writes=dkeys)

        rope_ctr = [0]

        def proj_v(wt, wkey, ncols, col0, n, dsts):
            wv = wt[:, 0:8 * ncols].rearrange("p (c n) -> p c n", n=ncols)
            for tb in range(4):
                b = next_bank()
                for c in range(8):
                    P.add("pe", lambda e, c=c, tb=tb, b=b: e.matmul(ps[b][:, 0:n], lhsT=xn[:, c, tb * 128:(tb + 1) * 128], rhs=wv[:, c, col0:col0 + n],
                                                                    start=(c == 0), stop=(c == 7)),
                          reads=[wkey, ("xn", c)], writes=[bk(b)])
                for (dst, dkeys, lo, hi) in dsts(tb):
                    eng = "act" if tb % 2 == 0 else "dve"
                    if eng == "act":
                        P.add("act", lambda e, dst=dst, b=b, lo=lo, hi=hi: e.copy(out=dst, in_=ps[b][:, lo:hi]), reads=[bk(b)], writes=dkeys)
                    else:
                        P.add("dve", lambda e, dst=dst, b=b, lo=lo, hi=hi: e.tensor_copy(out=dst, in_=ps[b][:, lo:hi]), reads=[bk(b)], writes=dkeys)

        def kslot(g, i):
            return i % NR[g]

        def kv_proj(i, groups):
            if any(g in groups for g in (0, 1)):
                wt, wkey = next_piece("ak0")
                for hd_ in range(4):
                    g, hh = hd_ // 2, hd_ % 2
                    proj_fm(wt, wkey, 512, hd_ * 128, True, lambda g=g, hh=hh: (Kr[g][:, kslot(g, i), hh, :], [("K", g, kslot(g, i), hh)]))
            wt, wkey = next_piece("ak1")
            for hh in range(2):
                proj_fm(wt, wkey, 256, hh * 128, True, lambda hh=hh: (Kr[2][:, kslot(2, i), hh, :], [("K", 2, kslot(2, i), hh)]))
            if any(g in groups for g in (0, 1)):
                wt, wkey = next_piece("av0")
                proj_v(wt, wkey, 512, 0, 512,
                       lambda tb: [(Vr[0][:, kslot(0, i), tb, :], [("V", 0, kslot(0, i), tb)], 0, 256),
                                   (Vr[1][:, kslot(1, i), tb, :], [("V", 1, kslot(1, i), tb)], 256, 512)])
            wt, wkey = next_piece("av1")
            proj_v(wt, wkey, 256, 0, 256, lambda tb: [(Vr[2][:, kslot(2, i), tb, :], [("V", 2, kslot(2, i), tb)], 0, 256)])

        def bkv_proj(i, wt, wkey):
            for hh in range(2):
                proj_fm(wt, wkey, 512, hh * 128, True, lambda hh=hh: (Kr[3][:, kslot(3, i), hh, :], [("K", 3, kslot(3, i), hh)]))
            proj_v(wt, wkey, 512, 256, 256, lambda tb: [(Vr[3][:, kslot(3, i), tb, :], [("V", 3, kslot(3, i), tb)], 0, 256)])

        def blocks_for(i, g):
            out = []
            if g in (0, 3):
                mo = M_A1 if g == 0 else M_B
                out.append((i - 1, 3, 0, 128, mo + 128))
                for jk in range(4):
                    q0, q1 = 128 * jk, min(512, 128 * (jk + 2))
                    out.append((i, jk, q0, q1, mo))
            elif g == 1:
                for jk in range(4):
                    out.append((i - 1, jk, 0, 128 * (jk + 1), M_G2PREV + 128 * (3 - jk)))
                for jk in range(4):
                    out.append((i, jk, 128 * jk, 512, M_G2OWN))
            else:
                for it in (i - 3, i - 2, i - 1):
                    for jk in range(4):
                        out.append((it, jk, 0, 512, M_G3ALL))
                for jk in range(4):
                    out.append((i - 4, jk, 0, 128 * (jk + 1), M_G3PREV + 128 * (3 - jk)))
                for jk in range(4):
                    out.append((i, jk, 128 * jk, 512, M_G3OWN))
            return out

        pb_ctr = [0]

        def attend(work, nb, db, finish):
            P.add("pe", lambda e: e.matmul(ps[nb][:], lhsT=zeros_b[:], rhs=xn[:, 0, :], start=True, stop=False),
                  reads=["zeros_b", ("xn", 0)], writes=[bk(nb)])
            P.add("pe", lambda e: e.matmul(ps[db][:], lhsT=zeros_b[:], rhs=xn[:, 0, :], start=True, stop=False),
                  reads=["zeros_b", ("xn", 0)], writes=[bk(db)])
            staged = []

            def stage_s(w):
                n = w["q1"] - w["q0"]
                sbank = next_bank(pool=(0, 1, 2, 3))
                k = pb_ctr[0] % 4
                pb_ctr[0] += 1
                P.add("pe", lambda e: e.matmul(ps[sbank][:, 0:n], lhsT=w["kT"], rhs=w["qap"][:, w["q0"]:w["q1"]], start=True, stop=True),
                      reads=w["kkeys"] + [w["qkey"]], writes=[bk(sbank)])
                if w["halo"]:
                    P.add("act", lambda e: e.activation(out=Pb[:, k, 0:n], in_=ps[sbank][:, 0:n], func=AF.Exp, bias=hb[:, 0:1], scale=SCALE),
                          reads=[bk(sbank), "hb"], writes=[("Pb", k)])
                else:
                    P.add("act", lambda e: e.activation(out=Pb[:, k, 0:n], in_=ps[sbank][:, 0:n], func=AF.Exp, scale=SCALE),
                          reads=[bk(sbank)], writes=[("Pb", k)])
                if w.get("mask") is not None:
                    mo = w["mask"]
                    P.add("dve", lambda e: e.tensor_tensor(out=Pb[:, k, 0:n], in0=Pb[:, k, 0:n], in1=masks[:, mo:mo + n], op=ALU.mult),
                          reads=[("Pb", k), "masks"], writes=[("Pb", k)])
                staged.append((w, k, n))

            def stage_pv(last):
                w, k, n = staged.pop(0)
                P.add("pe", lambda e: e.matmul(ps[nb][:, w["q0"]:w["q1"]], lhsT=w["vT"], rhs=Pb[:, k, 0:n], start=False, stop=last),
                      reads=w["vkeys"] + [("Pb", k)], writes=[bk(nb)])
                P.add("pe", lambda e: e.matmul(ps[db][:, w["q0"]:w["q1"]], lhsT=ones_b[:], rhs=Pb[:, k, 0:n], start=False, stop=last),
                      reads=["ones_b", ("Pb", k)], writes=[bk(db)])

            LOOK = 3
            for idx, w in enumerate(work):
                stage_s(w)
                if idx >= LOOK:
                    stage_pv(False)
            while staged:
                stage_pv(len(staged) == 1)
            finish(nb, db)

        def finish_plain(oc):
            def fin(nb, db):
                P.add("dve", lambda e: e.reciprocal(out=rec, in_=ps[db][:]), reads=[bk(db)], writes=["rec"])
                P.add("dve", lambda e: e.tensor_tensor(out=oT[:, oc, :], in0=ps[nb][:], in1=rec, op=ALU.mult),
                      reads=[bk(nb), "rec"], writes=[("oT", oc)])
            return fin

        def finish_sink(oc, h):
            def fin(nb, db):
                P.add("dve", lambda e: e.tensor_scalar(out=rec, in0=ps[db][:], scalar1=esink[:, h:h + 1], scalar2=None, op0=ALU.add),
                      reads=[bk(db), "esink"], writes=["rec"])
                P.add("dve", lambda e: e.reciprocal(out=rec, in_=rec), reads=["rec"], writes=["rec"])
                P.add("dve", lambda e: e.tensor_tensor(out=oT[:, oc, :], in0=ps[nb][:], in1=rec, op=ALU.mult),
                      reads=[bk(nb), "rec"], writes=[("oT", oc)])
            return fin

        nd_ctr = [0]

        def nd_banks():
            k = nd_ctr[0] % 2
            nd_ctr[0] += 1
            return (4, 5) if k == 0 else (6, 7)

        def mixer(i):
            hch = [hbuf[:, c, :] for c in range(8)]
            hkeys = [("h", c) for c in range(8)]
            P.seed(MIX_KEYS, FFN_KEYS)
            rms_stats(hch, hkeys)
            normalize(hch, hkeys, 2)
            wt, wkey = next_piece("aq0")
            for hd_ in range(4):
                proj_fm(wt, wkey, 512, hd_ * 128, True, lambda hd_=hd_: (qT[:, hd_, :], ["qT%d" % hd_]))
            wt, wkey = next_piece("aq1")
            for hd_ in range(2):
                proj_fm(wt, wkey, 256, hd_ * 128, True, lambda hd_=hd_: (qT[:, 4 + hd_, :], ["qT%d" % (4 + hd_)]))
            kv_proj(i, (0, 1, 2))
            if stage == 21:
                return
            for hh in range(2):
                work = []
                for g in (2, 1, 0):
                    for (it, jk, q0, q1, mo) in blocks_for(i, g):
                        s = kslot(g, it)
                        work.append(dict(kT=Kr[g][:, s, hh, jk * 128:(jk + 1) * 128], kkeys=[("K", g, s, hh)],
                                         vT=Vr[g][:, s, jk, hh * 128:(hh + 1) * 128], vkeys=[("V", g, s, jk)],
                                         qap=qT[:, 2 * g + hh, :], qkey="qT%d" % (2 * g + hh), q0=q0, q1=q1, mask=mo, halo=(it < NHALO)))
                nb, db = nd_banks()
                attend(work, nb, db, finish_plain(hh))
            if stage == 22:
                return
            wt, wkey = next_piece("bq")
            for hd_ in range(4):
                proj_fm(wt, wkey, 512, hd_ * 128, True, lambda hd_=hd_: (qT[:, hd_, :], ["qT%d" % hd_]))
            wt, wkey = next_piece("bkv")
            bkv_proj(i, wt, wkey)
            for h in range(4):
                kvh = h // 2
                work = []
                for (it, jk, q0, q1, mo) in blocks_for(i, 3):
                    s = kslot(3, it)
                    work.append(dict(kT=Kr[3][:, s, kvh, jk * 128:(jk + 1) * 128], kkeys=[("K", 3, s, kvh)],
                                     vT=Vr[3][:, s, jk, kvh * 128:(kvh + 1) * 128], vkeys=[("V", 3, s, jk)],
                                     qap=qT[:, h, :], qkey="qT%d" % h, q0=q0, q1=q1, mask=mo, halo=(it < NHALO)))
                nb, db = nd_banks()
                attend(work, nb, db, finish_sink(2 + h, h))
            if stage == 23:
                return
            wt, wkey = next_piece("mq")
            for hd_ in range(4):
                proj_fm(wt, wkey, 512, hd_ * 128, False, lambda hd_=hd_: (qT[:, hd_, :], ["qT%d" % hd_]))
            for h in range(4):
                work = []
                for b_ in range(2):
                    work.append(dict(kT=mkT[:, h, b_ * 128:(b_ + 1) * 128], kkeys=["mkT"],
                                     vT=mv[:, b_, h * 128:(h + 1) * 128], vkeys=["mv"],
                                     qap=qT[:, h, :], qkey="qT%d" % h, q0=0, q1=512, mask=None, halo=False))
                nb, db = nd_banks()
                attend(work, nb, db, finish_plain(6 + h))
            if stage == 24:
                return
            P.seed([("merged", k) for k in range(8)], ["qT%d" % k for k in range(6)] + [("xb", 0), ("xb", 1)])
            order = [(2, 0), (4, 2), (4, 6)]
            for br, (nk, oc0) in enumerate(order):
                for gi_ in range(2):
                    wt_g, wkey_g = next_piece("mg%d%d" % (br, gi_))
                    wg_v = wt_g[:, 0:4096].rearrange("p (c n) -> p c n", n=512)
                    wo_v = wt_g[:, 4096:4096 + nk * 512].rearrange("p (c n) -> p c n", n=512)
                    for o4 in range(4):
                        o = gi_ * 4 + o4
                        bp, bgz = next_bank(), next_bank()
                        for kc in range(nk):
                            P.add("pe", lambda e, kc=kc, bp=bp, o4=o4, wo_v=wo_v, oc0=oc0, nk=nk: e.matmul(ps[bp][:], lhsT=wo_v[:, kc, o4 * 128:(o4 + 1) * 128], rhs=oT[:, oc0 + kc, :],
                                                                              start=(kc == 0), stop=(kc == nk - 1)),
                                  reads=[wkey_g, ("oT", oc0 + kc)], writes=[bk(bp)])
                        for c in range(8):
                            P.add("pe", lambda e, c=c, bgz=bgz, o4=o4, wg_v=wg_v: e.matmul(ps[bgz][:], lhsT=wg_v[:, c, o4 * 128:(o4 + 1) * 128], rhs=xn[:, c, :],
                                                                                start=(c == 0), stop=(c == 7)),
                                  reads=[wkey_g, ("xn", c)], writes=[bk(bgz)])
                        sk = sig_ctr[0] % 2
                        sig_ctr[0] += 1
                        gcol = br * 8 + o
                        P.add("act", lambda e, sk=sk, bgz=bgz, gcol=gcol: e.activation(out=sig[:, sk, :], in_=ps[bgz][:], func=AF.Sigmoid,
                                                                                      bias=bgate[:, gcol:gcol + 1], scale=1.0),
                              reads=[bk(bgz), "bgate"], writes=[("sig", sk)])
                        if br == 0:
                            P.add("dve", lambda e, sk=sk, bp=bp, o=o: e.tensor_tensor(out=fbuf[:, o, :], in0=sig[:, sk, :], in1=ps[bp][:], op=ALU.mult),
                                  reads=[("sig", sk), bk(bp)], writes=[("f", o)])
                        else:
                            P.add("dve", lambda e, sk=sk, bp=bp: e.tensor_tensor(out=prod, in0=sig[:, sk, :], in1=ps[bp][:], op=ALU.mult),
                                  reads=[("sig", sk), bk(bp)], writes=["prod"])
                            if br == 1:
                                P.add("dve", lambda e, o=o: e.tensor_tensor(out=fbuf[:, o, :], in0=fbuf[:, o, :], in1=prod, op=ALU.add),
                                      reads=[("f", o), "prod"], writes=[("f", o)])
                            else:
                                P.add("dve", lambda e, o=o: e.tensor_tensor(out=merged[:, o, :], in0=fbuf[:, o, :], in1=prod, op=ALU.add),
                                      reads=[("f", o), "prod"], writes=[("merged", o)])
            for m in range(2):
                wt, wkey = next_piece("wo%d" % m)
                wv = wt[:, 0:4096].rearrange("p (c n) -> p c n", n=512)
                for o4 in range(4):
                    o = m * 4 + o4
                    b = next_bank()
                    for c in range(8):
                        P.add("pe", lambda e, c=c, b=b, o4=o4, wv=wv: e.matmul(ps[b][:], lhsT=wv[:, c, o4 * 128:(o4 + 1) * 128], rhs=merged[:, c, :],
                                                                        start=(c == 0), stop=(c == 7)),
                              reads=[wkey, ("merged", c)], writes=[bk(b)])
                    stats_square(o, ps[b][:], [bk(b)])
                    P.add("dve", lambda e, o=o, b=b: e.tensor_copy(out=fbuf[:, o, :], in_=ps[b][:]), reads=[bk(b), ("sq8", o)], writes=[("f", o)])
            stats_finish()
            for c in range(8):
                P.add("dve", lambda e, c=c: e.scalar_tensor_tensor(out=fbuf[:, c, :], in0=fbuf[:, c, :], scalar=gains[:, 3, c:c + 1],
                                                                   in1=rstd[:], op0=ALU.mult, op1=ALU.mult),
                      reads=[("f", c), "gains", "rstd"], writes=[("f", c)])
                P.add("dve", lambda e, c=c: e.tensor_tensor(out=hbuf[:, c, :], in0=fbuf[:, c, :], in1=hbuf[:, c, :], op=ALU.add),
                      reads=[("f", c), ("h", c)], writes=[("h", c)])

        sig_ctr = [0]

        load_x(tiles[0])
        load_cs(tiles[0])
        for k2, nm in enumerate(("mem0", "mem1")):
            c0 = 512 * k2
            o_ = wslot[1 + k2][:, 0:4096].rearrange("p (c n) -> p c n", n=512)
            s_ = dr["w_mem_kv"][:, c0:c0 + 512].rearrange("(c p) n -> p c n", p=128)
            P.add("pool", lambda e, o_=o_, s_=s_: e.dma_start(out=o_, in_=s_), writes=[("wslot", 1 + k2)], dma_key=("memw", k2))
        P.add("sp", lambda e: e.dma_start(out=fbuf[:, :, 0:256], in_=memT.rearrange("(c p) t -> p c t", p=128)),
              writes=[("f", c) for c in range(8)], dma_key="memload")
        GSZ = 3
        for k_, nm in enumerate(used_pieces):
            piece_group[nm] = ("wscg", k_ // GSZ)
            cast_piece(nm, wsc[pid[nm]], ("wsc", nm), piece_group[nm])
        mch = [fbuf[:, c, 0:256] for c in range(8)]
        mkeys = [("f", c) for c in range(8)]
        rms_stats(mch, mkeys, ncols=256)
        normalize(mch, mkeys, 6, ncols=256)
        wk_v = wslot[1][:, 0:4096].rearrange("p (c n) -> p c n", n=512)
        wv_v = wslot[2][:, 0:4096].rearrange("p (c n) -> p c n", n=512)
        for h in range(4):
            b = next_bank()
            for c in range(8):
                P.add("pe", lambda e, c=c, b=b, h=h: e.matmul(ps[b][:, 0:256], lhsT=wk_v[:, c, h * 128:(h + 1) * 128], rhs=xn[:, c, 0:256],
                                                              start=(c == 0), stop=(c == 7)),
                      reads=[("wslot", 1), ("xn", c)], writes=[bk(b)])
            P.add("act", lambda e, b=b, h=h: e.copy(out=mkT[:, h, :], in_=ps[b][:, 0:256]), reads=[bk(b)], writes=["mkT"])
        for tb in range(2):
            b = next_bank()
            for c in range(8):
                P.add("pe", lambda e, c=c, b=b, tb=tb: e.matmul(ps[b][:], lhsT=xn[:, c, tb * 128:(tb + 1) * 128], rhs=wv_v[:, c, :],
                                                                start=(c == 0), stop=(c == 7)),
                      reads=[("wslot", 2), ("xn", c)], writes=[bk(b)])
            P.add("dve", lambda e, b=b, tb=tb: e.tensor_copy(out=mv[:, tb, :], in_=ps[b][:]), reads=[bk(b)], writes=["mv"])

        xin_keys = ["xin"] * 8
        for ti, i in enumerate(tiles):
            ffn(1, xin, xin_keys, xin, xin_keys)
            nxt = tiles[ti + 1] if ti + 1 < len(tiles) else None
            if i < NHALO:
                hch = [hbuf[:, c, :] for c in range(8)]
                hkeys = [("h", c) for c in range(8)]
                P.seed(MIX_KEYS, FFN_KEYS)
                rms_stats(hch, hkeys)
                normalize(hch, hkeys, 2)
                if i == NHALO - 1:
                    kv_proj(i, (0, 1, 2))
                    wt, wkey = next_piece("bkv")
                    bkv_proj(i, wt, wkey)
                else:
                    kv_proj(i, (2,))
                if nxt is not None:
                    load_x(nxt)
                    load_cs(nxt)
                continue
            if stage == 1:
                store_y(i)
                if nxt is not None:
                    load_x(nxt)
                continue
            mixer(i)
            if nxt is not None:
                load_cs(nxt)
            if stage >= 2:
                store_y(i)
                if nxt is not None:
                    load_x(nxt)
                continue
            hb_ = hbuf
            hk_ = [("h", c) for c in range(8)]
            if nxt is not None:
                load_x(nxt)
            ffn(2, hb_, hk_, hb_, hk_)
            store_y(i)
        P.add("sp", None, reads=[("y", i) for i in tiles if i >= NHALO])
        P.emit_all(nc)
    return nc, P


def host_constants():
    k = np.arange(128)[:, None]
    q = np.arange(128)[None, :]
    GE = (q >= k)
    LE = (q <= k)
    LT = (q < k)
    M4 = ((q - k) % 4 == 0)
    M16 = ((q - k) % 16 == 0)
    m = np.zeros((128, NMASK), np.float32)
    m[:, M_A1:M_A1 + 256] = np.concatenate([GE, LE], 1)
    m[:, M_B:M_B + 256] = np.concatenate([GE, LT], 1)
    m[:, M_G2OWN:M_G2OWN + 512] = np.concatenate([M4 & GE, M4, M4, M4], 1)
    m[:, M_G2PREV:M_G2PREV + 512] = np.concatenate([M4, M4, M4, M4 & LE], 1)
    m[:, M_G3OWN:M_G3OWN + 512] = np.concatenate([M16 & GE, M16, M16, M16], 1)
    m[:, M_G3ALL:M_G3ALL + 512] = np.concatenate([M16, M16, M16, M16], 1)
    m[:, M_G3PREV:M_G3PREV + 512] = np.concatenate([M16, M16, M16, M16 & LE], 1)
    rp = np.zeros((128, 128), np.float32)
    for mm in range(128):
        rp[(mm + 64) % 128, mm] = 1.0
    return m, rp


def chunked(v):
    return np.ascontiguousarray(np.asarray(v, np.float32).reshape(8, 128).T)


_CACHE = {}


def make_in_maps(inputs):
    x = np.asarray(inputs["x"], np.float32)
    mem = np.asarray(inputs["mem"], np.float32)
    masks, rp = host_constants()
    gains = np.stack([chunked(inputs[n][0]) for n in ("ffn1_norm_pre", "ffn1_norm_post", "mix_norm_pre", "mix_norm_post",
                                                      "ffn2_norm_pre", "ffn2_norm_post", "mem_norm")], axis=1)
    gains = np.ascontiguousarray(gains, np.float32)
    bg = np.ascontiguousarray(np.asarray(inputs["b_gate"][0], np.float32).reshape(24, 128).T)
    sinks = np.ascontiguousarray(np.asarray(inputs["sinks"], np.float32).reshape(1, 4))
    wmap = {n: np.ascontiguousarray(np.asarray(inputs[n][0], np.float32)) for n in WNAMES}
    inv = (np.float32(10000.0) ** (-(np.arange(64, dtype=np.float32) / np.float32(64)))).astype(np.float32)
    in_maps = []
    for c in range(8):
        b, a = c // 4, c % 4
        t0 = a * NOWN * TT
        lo = t0 - NHALO * TT
        xt = np.zeros((D, NTOK), np.float32)
        if lo >= 0:
            xt[:, :] = x[b, lo:t0 + NOWN * TT, :].T
        else:
            xt[:, NHALO * TT:] = x[b, 0:NOWN * TT, :].T
        pos = np.arange(lo, t0 + NOWN * TT).astype(np.float32)
        ang = pos[:, None] * inv[None, :]
        cos = np.cos(ang).astype(np.float32).T
        sin = np.sin(ang).astype(np.float32).T
        cs = np.empty((128, 2, NTOK), np.float32)
        cs[0:64, 0] = cos
        cs[64:128, 0] = cos
        cs[0:64, 1] = -sin
        cs[64:128, 1] = sin
        hbv = np.full((128, 1), 0.0 if a > 0 else -30000.0, np.float32)
        m = dict(wmap)
        m.update(xT=xt, cs=cs, hb=hbv, memT=np.ascontiguousarray(mem[b].T), masks=masks, rperm=rp, gains=gains,
                 bgate=bg, sinks=sinks)
        in_maps.append(m)
    return in_maps


def kernel(**inputs):
    if "nc" not in _CACHE:
        _CACHE["nc"] = build_program()[0]
    nc = _CACHE["nc"]
    in_maps = make_in_maps(inputs)
    res = run_bass_kernel_spmd(nc, in_maps, core_ids=list(range(8)))
    out = np.empty((2, 16384, D), np.float32)
    for c in range(8):
        b, a = c // 4, c % 4
        t0 = a * NOWN * TT
        out[b, t0:t0 + NOWN * TT, :] = res.results[c]["yT"].T
    return out
```

```python
import contextlib
import numpy as np
import concourse.bass as bass
import concourse.mybir as mybir
from concourse.bass_utils import run_bass_kernel_spmd

F32 = mybir.dt.float32
BF16 = mybir.dt.bfloat16
ALU = mybir.AluOpType
AF = mybir.ActivationFunctionType

D = 1024
DFF = 2816
NF = 22
TT = 512
NHALO = 4
NOWN = 8
NTILE = NHALO + NOWN
NTOK = NTILE * TT
EPS = 1e-6
SCALE = 128 ** -0.5
SLOT = 6144
NSLOT = 4
ROPE_ADD_ENG = "pool"

M_A1, M_B, M_G2OWN, M_G2PREV, M_G3OWN, M_G3ALL, M_G3PREV = 0, 256, 512, 1024, 1536, 2048, 2560
NMASK = 3072


class Op:
    __slots__ = ("eng", "emit", "deps", "token", "needs_token", "is_dma", "dma_key", "epoch", "idx")


class Prog:
    ENGS = ("pe", "act", "dve", "pool", "sp")

    def __init__(self):
        self.ops = []
        self.lastw = {}
        self.readers = {}
        self.epoch = 0

    def new_epoch(self):
        self.epoch += 1

    def seed(self, new_keys, old_keys):
        ops = {}
        for o in old_keys:
            w = self.lastw.get(o)
            if w is not None:
                ops[w.idx] = w
            for d in self.readers.get(o, {}).values():
                ops[d.idx] = d
        for k in new_keys:
            rd = self.readers.setdefault(k, {})
            for idx, op in ops.items():
                rd[("seed", idx)] = op

    def add(self, eng, emit, reads=(), writes=(), dma_key=None, after=()):
        op = Op()
        op.eng = eng
        op.emit = emit
        op.is_dma = dma_key is not None
        op.dma_key = dma_key
        op.needs_token = op.is_dma
        op.token = None
        op.epoch = self.epoch
        op.idx = len(self.ops)
        deps = {}
        for r in reads:
            w = self.lastw.get(r)
            if w is not None:
                deps[w.idx] = w
        for k in writes:
            w = self.lastw.get(k)
            if w is not None:
                deps[w.idx] = w
            rd = self.readers.get(k)
            if rd:
                for d in rd.values():
                    deps[d.idx] = d
        for d in after:
            deps[d.idx] = d
        op.deps = []
        for d in deps.values():
            if d.eng == "pe" and eng == "pe" and not d.is_dma and not op.is_dma:
                continue
            d.needs_token = True
            op.deps.append(d)
        for r in reads:
            rd = self.readers.setdefault(r, {})
            rd[("dma", op.idx) if op.is_dma else eng] = op
        for k in writes:
            self.lastw[k] = op
            self.readers[k] = {}
        self.ops.append(op)
        return op

    def emit_all(self, nc):
        with contextlib.ExitStack() as st:
            sems = {}

            def getsem(key):
                if key not in sems:
                    sems[key] = st.enter_context(nc.semaphore("s%d" % len(sems)))
                return sems[key]

            counts = {}
            for op in self.ops:
                if op.is_dma:
                    key = ("dma", op.dma_key)
                    counts[key] = counts.get(key, 0) + 16
                    op.token = (getsem(key), counts[key])
                elif op.needs_token:
                    key = (op.eng, op.epoch)
                    counts[key] = counts.get(key, 0) + 1
                    op.token = (getsem(key), counts[key])
            self.n_sems = len(sems)
            self.sem_counts = counts
            block = st.enter_context(nc.Block())
            per = {e: [o for o in self.ops if o.eng == e] for e in self.ENGS}

            def run(e, ops):
                known = {}
                for op in ops:
                    waits = {}
                    for d in op.deps:
                        sem, val = d.token
                        sid = id(sem)
                        if known.get(sid, 0) < val:
                            if sid not in waits or waits[sid][1] < val:
                                waits[sid] = (sem, val)
                    for sid, (sem, val) in waits.items():
                        e.wait_ge(sem, val)
                        known[sid] = val
                    if op.emit is None:
                        continue
                    ins = op.emit(e)
                    if op.token is not None:
                        ins.then_inc(op.token[0], 16 if op.is_dma else 1)

            @block.tensor
            def _(e):
                run(e, per["pe"])

            @block.scalar
            def _(e):
                run(e, per["act"])

            @block.vector
            def _(e):
                run(e, per["dve"])

            @block.gpsimd
            def _(e):
                run(e, per["pool"])

            @block.sync
            def _(e):
                run(e, per["sp"])


def piece_table():
    t = {}
    for w in (1, 2):
        for j in range(11):
            t["f%d_in%d" % (w, j)] = [("cols", "ffn%d_w_in" % w, 256 * j, 256, 0),
                                      ("cols", "ffn%d_w_in" % w, DFF + 256 * j, 256, 2048)]
        for m in range(4):
            t["f%d_out%d" % (w, m)] = [("ocols", "ffn%d_w_out" % w, 256 * m, 256, 0)]
    t["aq0"] = [("cols", "w_in", 0, 512, 0)]
    t["aq1"] = [("cols", "w_in", 512, 256, 0)]
    t["ak0"] = [("cols", "w_in", 768, 512, 0)]
    t["ak1"] = [("cols", "w_in", 1280, 256, 0)]
    t["av0"] = [("cols", "w_in", 1536, 512, 0)]
    t["av1"] = [("cols", "w_in", 2048, 256, 0)]
    t["bq"] = [("cols", "w_in", 2304, 512, 0)]
    t["bkv"] = [("cols", "w_in", 2816, 512, 0)]
    t["mq"] = [("cols", "w_in", 3328, 512, 0)]
    for br, src in enumerate(("w_o_a", "w_o_b", "w_o_m")):
        for half in range(2):
            t["mg%d%d" % (br, half)] = [("cols", "w_gate", 1024 * br + 512 * half, 512, 0),
                                        ("rcols", src, 512 * half, 512, 4096)]
    t["wo0"] = [("cols", "w_out", 0, 512, 0)]
    t["wo1"] = [("cols", "w_out", 512, 512, 0)]
    return t


def part_elems(part):
    kind, _, _, n, _ = part
    if kind == "cols":
        return 8 * n
    if kind == "ocols":
        return NF * n
    return (WSHAPES[part[1]][0] // 128) * n


def piece_used(parts):
    return max(p[4] + part_elems(p) for p in parts)


def tile_pieces(i, stage=0):
    ffn1 = ["f1_in%d" % j for j in range(11)] + ["f1_out%d" % m for m in range(4)]
    if i < NHALO - 1:
        return ffn1 + ["ak1", "av1"]
    if i == NHALO - 1:
        return ffn1 + ["ak0", "ak1", "av0", "av1", "bkv"]
    if stage == 1:
        return ffn1
    mix = ["aq0", "aq1", "ak0", "ak1", "av0", "av1", "bq", "bkv", "mq",
           "mg00", "mg01", "mg10", "mg11", "mg20", "mg21", "wo0", "wo1"]
    ffn2 = ["f2_in%d" % j for j in range(11)] + ["f2_out%d" % m for m in range(4)]
    if stage in (21, 22):
        return ffn1 + mix[:6]
    if stage == 23:
        return ffn1 + mix[:8]
    if stage == 24:
        return ffn1 + mix[:9]
    if stage == 2:
        return ffn1 + mix
    return ffn1 + mix + ffn2


WNAMES = ["ffn1_w_in", "ffn1_w_out", "w_in", "w_mem_kv", "w_gate", "w_o_a", "w_o_b", "w_o_m", "w_out",
          "ffn2_w_in", "ffn2_w_out"]
WSHAPES = {"ffn1_w_in": [D, 2 * DFF], "ffn1_w_out": [DFF, D], "w_in": [D, 3840], "w_mem_kv": [D, D],
           "w_gate": [D, 3 * D], "w_o_a": [256, D], "w_o_b": [512, D], "w_o_m": [512, D], "w_out": [D, D],
           "ffn2_w_in": [D, 2 * DFF], "ffn2_w_out": [DFF, D]}


def build_program(first_tile=0, last_tile=NTILE, stage=0):
    nc = bass.Bass("TRN2", target_bir_lowering=False)
    dr = {}
    for name in WNAMES:
        dr[name] = nc.dram_tensor(name, WSHAPES[name], F32, kind="ExternalInput").ap()
    xT = nc.dram_tensor("xT", [D, NTOK], F32, kind="ExternalInput").ap()
    csd = nc.dram_tensor("cs", [128, 2, NTOK], F32, kind="ExternalInput").ap()
    hbd = nc.dram_tensor("hb", [128, 1], F32, kind="ExternalInput").ap()
    memT = nc.dram_tensor("memT", [D, 256], F32, kind="ExternalInput").ap()
    masksd = nc.dram_tensor("masks", [128, NMASK], F32, kind="ExternalInput").ap()
    rpermd = nc.dram_tensor("rperm", [128, 256], F32, kind="ExternalInput").ap()
    gainsd = nc.dram_tensor("gains", [128, 7, 8], F32, kind="ExternalInput").ap()
    bgated = nc.dram_tensor("bgate", [128, 24], F32, kind="ExternalInput").ap()
    sinksd = nc.dram_tensor("sinks", [1, 4], F32, kind="ExternalInput").ap()
    yT = nc.dram_tensor("yT", [D, NOWN * TT], F32, kind="ExternalOutput").ap()

    ptab = piece_table()
    pnames = list(ptab.keys())
    pid = {n: k for k, n in enumerate(pnames)}
    wsc = nc.dram_tensor("wsc", [len(pnames), 128, SLOT], BF16).ap()

    P = Prog()
    st = contextlib.ExitStack()
    with st:
        def sb(name, shape, dt):
            return st.enter_context(nc.sbuf_tensor(name, shape, dt))

        ones_f = sb("ones_f", [128, 128], F32)
        ones_b = sb("ones_b", [128, 128], BF16)
        zeros_b = sb("zeros_b", [128, 128], BF16)
        rperm = sb("rperm_sb", [128, 256], BF16)
        masks = sb("masks_sb", [128, NMASK], BF16)
        gains = sb("gains_sb", [128, 7, 8], F32)
        ghalf = sb("ghalf_sb", [128, 2, 8], F32)
        bgate = sb("bgate_sb", [128, 24], F32)
        esink = sb("esink_sb", [128, 4], F32)
        hb = sb("hb_sb", [128, 1], F32)
        hbuf = sb("hbuf", [128, 8, TT], F32)
        xn = sb("xn", [128, 8, TT], BF16)
        sq8 = sb("sq8", [128, 8, TT], BF16)
        ones_s = sb("ones_s", [128, 128], BF16)
        lnm = sb("lnm", [128, TT], F32)
        rstd = sb("rstd", [128, TT], F32)
        fbuf = sb("fbuf", [128, 8, TT], F32)
        mkT = sb("mkT", [128, 4, 256], BF16)
        mv = sb("mv", [128, 2, 512], BF16)
        NR = {0: 2, 1: 2, 2: 5, 3: 2}
        Kr = {g: sb("K%d" % g, [128, NR[g], 2, TT], BF16) for g in range(4)}
        Vr = {g: sb("V%d" % g, [128, NR[g], 4, 256], BF16) for g in range(4)}
        wslot = [sb("wslot%d" % s, [128, SLOT], BF16) for s in range(NSLOT)]
        scr = sb("scr", [128, 24576], BF16)
        ps = [st.enter_context(nc.psum_tensor("ps%d" % k, [128, TT], F32)) for k in range(8)]

        def f32view(lo_b, hi_b):
            return scr[:, lo_b // 2:hi_b // 2].bitcast(F32)

        hid = scr[:, 0:NF * TT].rearrange("p (c t) -> p c t", t=TT)
        xin = f32view(22528, 38912).rearrange("p (c t) -> p c t", t=TT)
        sg = f32view(38912, 43008).rearrange("p (c t) -> p c t", t=TT)
        qT = scr[:, 0:3072].rearrange("p (c t) -> p c t", t=TT)
        xb = scr[:, 3072:4096].rearrange("p (c t) -> p c t", t=TT)
        rt = f32view(8192, 16384).rearrange("p (c t) -> p c t", t=TT)
        Pb = scr[:, 10240:12288].rearrange("p (c t) -> p c t", t=TT)
        oT = scr[:, 12288:17408].rearrange("p (c t) -> p c t", t=TT)
        rec = f32view(34816, 36864)
        sig = f32view(36864, 40960).rearrange("p (c t) -> p c t", t=TT)
        prod = f32view(40960, 43008)
        merged = scr[:, 0:4096].rearrange("p (c t) -> p c t", t=TT)
        cst = f32view(43008, 47104).rearrange("p (c t) -> p c t", t=TT)

        FFN_KEYS = [("hid", f) for f in range(NF)] + ["xin", ("sg", 0), ("sg", 1)]
        MIX_KEYS = ["qT0", "qT1", "qT2", "qT3", "qT4", "qT5", ("xb", 0), ("xb", 1)] + [("rt", k) for k in range(4)] + \
                   [("Pb", k) for k in range(4)] + [("oT", k) for k in range(10)] + ["rec", ("sig", 0), ("sig", 1), "prod"] + \
                   [("merged", k) for k in range(8)]

        bank_ctr = [0]

        def next_bank(pool=(0, 1, 2, 3, 4, 5, 6, 7)):
            b = pool[bank_ctr[0] % len(pool)]
            bank_ctr[0] += 1
            return b

        def bk(b):
            return ("ps", b)

        P.add("sp", lambda e: e.dma_start(out=gains[:], in_=gainsd), writes=["gains"], dma_key="c_gains")
        P.add("sp", lambda e: e.dma_start(out=bgate[:], in_=bgated), writes=["bgate"], dma_key="c_bgate")
        P.add("sp", lambda e: e.dma_start(out=hb[:], in_=hbd), writes=["hb"], dma_key="c_hb")
        P.add("sp", lambda e: e.dma_start(out=esink[:], in_=sinksd.partition_broadcast(128)), writes=["esink"], dma_key="c_sink")
        for hh_ in range(2):
            P.add("pool", lambda e, hh_=hh_: e.dma_start(out=masks[:, hh_ * 1536:(hh_ + 1) * 1536], in_=masksd[:, hh_ * 1536:(hh_ + 1) * 1536]),
                  writes=["masks"], dma_key="c_masks")
        for g_ in range(4):
            P.add("pool", lambda e, g_=g_: e.memset(Kr[g_][:], 0.0), writes=[("K", g_, s_, h_) for s_ in range(NR[g_]) for h_ in range(2)])
            P.add("pool", lambda e, g_=g_: e.memset(Vr[g_][:], 0.0), writes=[("V", g_, s_, t_) for s_ in range(NR[g_]) for t_ in range(4)])
        P.add("pool", lambda e: e.dma_start(out=rperm[:], in_=rpermd), writes=["rperm"], dma_key="c_rperm")
        P.add("dve", lambda e: e.memset(ones_s[:], 1.0 / D), writes=["ones_s"])
        P.add("dve", lambda e: e.memset(ones_b[:], 1.0), writes=["ones_b"])
        P.add("dve", lambda e: e.memset(zeros_b[:], 0.0), writes=["zeros_b"])
        P.add("act", lambda e: e.activation(out=esink[:], in_=esink[:], func=AF.Exp), reads=["esink"], writes=["esink"])
        P.add("dve", lambda e: e.tensor_scalar(out=ghalf[:, 0, :], in0=gains[:, 1, :], scalar1=0.5, scalar2=None, op0=ALU.mult),
              reads=["gains"], writes=["ghalf"])
        P.add("dve", lambda e: e.tensor_scalar(out=ghalf[:, 1, :], in0=gains[:, 5, :], scalar1=0.5, scalar2=None, op0=ALU.mult),
              reads=["gains"], writes=["ghalf"])

        def load_x(i):
            P.seed(["xin"], MIX_KEYS)
            P.add("sp", lambda e: e.dma_start(out=xin, in_=xT[:, i * TT:(i + 1) * TT].rearrange("(c p) t -> p c t", p=128)),
                  writes=["xin"], dma_key="xin")

        def load_cs(i):
            P.add("sp", lambda e: e.dma_start(out=cst, in_=csd[:, :, i * TT:(i + 1) * TT]), writes=["cst"], dma_key="cst")

        def store_y(i):
            n = i - NHALO
            P.add("sp", lambda e: e.dma_start(out=yT[:, n * TT:(n + 1) * TT].rearrange("(c p) t -> p c t", p=128), in_=hbuf[:]),
                  reads=[("h", c) for c in range(8)], writes=[("y", i)], dma_key="ystore")

        tiles = list(range(first_tile, last_tile))
        seq = []
        for i in tiles:
            for nm in tile_pieces(i, stage):
                seq.append(nm)
        used_pieces = []
        for nm in seq:
            if nm not in used_pieces:
                used_pieces.append(nm)
        def cast_piece(nm, dst_img, key_prefix, dma_key):
            for k, part in enumerate(ptab[nm]):
                kind, src, c0, n, off = part
                W = dr[src]
                ne = part_elems(part)
                if kind == "cols":
                    o = dst_img[:, off:off + ne].rearrange("p (c n) -> p c n", n=n)
                    s_ = W[:, c0:c0 + n].rearrange("(c p) n -> p c n", p=128)
                elif kind == "ocols":
                    o = dst_img[:, off:off + ne].rearrange("p (c n) -> p c n", n=n)
                    s_ = W[:, c0:c0 + n].rearrange("(c p) n -> p c n", p=128)
                else:
                    o = dst_img[:, off:off + ne].rearrange("p (c n) -> p c n", n=n)
                    s_ = W[:, c0:c0 + n].rearrange("(c p) n -> p c n", p=128)
                thr = []
                prevg = ("wscg", dma_key[1] - 2)
                if prevg in last_cast:
                    thr = [last_cast[prevg]]
                last_cast[dma_key] = P.add("pool", lambda e, o=o, s_=s_: e.dma_start(out=o, in_=s_), writes=[(key_prefix, k)], dma_key=dma_key, after=thr)

        last_cast = {}
        piece_group = {}
        wstate = {"next_load": 0, "next_use": 0}

        def issue_loads(upto):
            while wstate["next_load"] <= upto and wstate["next_load"] < len(seq):
                k = wstate["next_load"]
                nm = seq[k]
                s = k % NSLOT
                used = piece_used(ptab[nm])
                src = wsc[pid[nm]]
                P.add("sp", lambda e, s=s, used=used, src=src: e.dma_start(out=wslot[s][:, 0:used], in_=src[:, 0:used]),
                      reads=[(("wsc", nm), kk) for kk in range(len(ptab[nm]))], writes=[("wslot", s)], dma_key=("wslot", s),
                      after=[last_cast[piece_group[nm]]])
                wstate["next_load"] += 1

        def next_piece(expect):
            k = wstate["next_use"]
            assert seq[k] == expect, (seq[k], expect)
            issue_loads(k + NSLOT - 1)
            wstate["next_use"] += 1
            s = k % NSLOT
            return wslot[s], ("wslot", s)

        def stats_square(c, src_ap, src_key_list, ncols=TT):
            P.add("act", lambda e: e.activation(out=sq8[:, c, 0:ncols], in_=src_ap, func=AF.Square),
                  reads=src_key_list, writes=[("sq8", c)])

        def stats_finish(ncols=TT):
            b = next_bank()
            for c in range(8):
                P.add("pe", lambda e, c=c: e.matmul(ps[b][:, 0:ncols], lhsT=ones_s[:], rhs=sq8[:, c, 0:ncols], start=(c == 0), stop=(c == 7)),
                      reads=["ones_s", ("sq8", c)], writes=[bk(b)])
            P.add("act", lambda e: e.activation(out=lnm[:, 0:ncols], in_=ps[b][:, 0:ncols], func=AF.Ln, bias=EPS, scale=1.0),
                  reads=[bk(b)], writes=["lnm"])
            P.add("act", lambda e: e.activation(out=rstd[:, 0:ncols], in_=lnm[:, 0:ncols], func=AF.Exp, scale=-0.5),
                  reads=["lnm"], writes=["rstd"])

        def rms_stats(src_chunks, src_keys, ncols=TT):
            for c in range(8):
                stats_square(c, src_chunks[c], [src_keys[c]], ncols)
            stats_finish(ncols)

        def normalize(src_chunks, src_keys, gidx, ncols=TT):
            for c in range(8):
                P.add("dve", lambda e, c=c: e.scalar_tensor_tensor(out=xn[:, c, 0:ncols], in0=src_chunks[c], scalar=gains[:, gidx, c:c + 1],
                                                                   in1=rstd[:, 0:ncols], op0=ALU.mult, op1=ALU.mult),
                      reads=[src_keys[c], "gains", "rstd"], writes=[("xn", c)])

        def ffn(which, src, src_keys, resid, resid_keys):
            src_chunks = [src[:, c, :] for c in range(8)]
            P.seed(FFN_KEYS[:NF] + FFN_KEYS[NF + 1:], MIX_KEYS)
            rms_stats(src_chunks, src_keys)
            normalize(src_chunks, src_keys, 0 if which == 1 else 4)
            for j in range(11):
                wt, wkey = next_piece("f%d_in%d" % (which, j))
                wg = wt[:, 0:2048].rearrange("p (c n) -> p c n", n=256)
                wu = wt[:, 2048:4096].rearrange("p (c n) -> p c n", n=256)
                for half in range(2):
                    f = 2 * j + half
                    bg, bu = next_bank(), next_bank()
                    for c in range(8):
                        P.add("pe", lambda e, c=c, bg=bg, wg=wg, half=half: e.matmul(ps[bg][:], lhsT=wg[:, c, half * 128:(half + 1) * 128], rhs=xn[:, c, :],
                                                                                     start=(c == 0), stop=(c == 7)),
                              reads=[wkey, ("xn", c)], writes=[bk(bg)])
                    for c in range(8):
                        P.add("pe", lambda e, c=c, bu=bu, wu=wu, half=half: e.matmul(ps[bu][:], lhsT=wu[:, c, half * 128:(half + 1) * 128], rhs=xn[:, c, :],
                                                                                     start=(c == 0), stop=(c == 7)),
                              reads=[wkey, ("xn", c)], writes=[bk(bu)])
                    P.add("act", lambda e, f=f, bg=bg: e.activation(out=sg[:, f % 2, :], in_=ps[bg][:], func=AF.Silu),
                          reads=[bk(bg)], writes=[("sg", f % 2)])
                    P.add("dve", lambda e, f=f, bu=bu: e.tensor_tensor(out=hid[:, f, :], in0=sg[:, f % 2, :], in1=ps[bu][:], op=ALU.mult),
                          reads=[("sg", f % 2), bk(bu)], writes=[("hid", f)])
            for m in range(4):
                wt, wkey = next_piece("f%d_out%d" % (which, m))
                wo = wt[:, 0:NF * 256].rearrange("p (c n) -> p c n", n=256)
                for oo in range(2):
                    o = 2 * m + oo
                    b = next_bank()
                    for f in range(NF):
                        P.add("pe", lambda e, f=f, b=b, wo=wo, oo=oo: e.matmul(ps[b][:], lhsT=wo[:, f, oo * 128:(oo + 1) * 128], rhs=hid[:, f, :],
                                                                               start=(f == 0), stop=(f == NF - 1)),
                              reads=[wkey, ("hid", f)], writes=[bk(b)])
                    stats_square(o, ps[b][:], [bk(b)])
                    P.add("dve", lambda e, o=o, b=b: e.tensor_copy(out=fbuf[:, o, :], in_=ps[b][:]), reads=[bk(b), ("sq8", o)], writes=[("f", o)])
            stats_finish()
            gi = 0 if which == 1 else 1
            for c in range(8):
                P.add("dve", lambda e, c=c: e.scalar_tensor_tensor(out=fbuf[:, c, :], in0=fbuf[:, c, :], scalar=ghalf[:, gi, c:c + 1],
                                                                   in1=rstd[:], op0=ALU.mult, op1=ALU.mult),
                      reads=[("f", c), "ghalf", "rstd"], writes=[("f", c)])
                P.add("dve", lambda e, c=c: e.tensor_tensor(out=hbuf[:, c, :], in0=fbuf[:, c, :], in1=resid[:, c, :], op=ALU.add),
                      reads=[("f", c), resid_keys[c]], writes=[("h", c)])

        def proj_fm(wt, wkey, ncols, col0, rope, dst_fn):
            wv = wt[:, 0:8 * ncols].rearrange("p (c n) -> p c n", n=ncols)
            b = next_bank()
            for c in range(8):
                P.add("pe", lambda e, c=c: e.matmul(ps[b][:], lhsT=wv[:, c, col0:col0 + 128], rhs=xn[:, c, :], start=(c == 0), stop=(c == 7)),
                      reads=[wkey, ("xn", c)], writes=[bk(b)])
            dst, dkeys = dst_fn()
            if not rope:
                P.add("act", lambda e: e.copy(out=dst, in_=ps[b][:]), reads=[bk(b)], writes=dkeys)
                return
            k = rope_ctr[0] % 2
            rope_ctr[0] += 1
            P.add("act", lambda e: e.copy(out=xb[:, k, :], in_=ps[b][:]), reads=[bk(b)], writes=[("xb", k)])
            prev = list(rope_pending)
            del rope_pending[:]
            rope_pending.append((b, k, dst, dkeys))
            for it in prev:
                rope_finish(*it)

        def rope_finish(b, k, dst, dkeys):
            b2 = next_bank()
            P.add("pe", lambda e: e.matmul(ps[b2][:], lhsT=rperm[:, 0:128], rhs=xb[:, k, :], start=True, stop=True),
                  reads=["rperm", ("xb", k)], writes=[bk(b2)])
            P.add("dve", lambda e: e.tensor_tensor(out=rt[:, 2 * k, :], in0=ps[b][:], in1=cst[:, 0, :], op=ALU.mult),
                  reads=[bk(b), "cst", ("xb", k)], writes=[("rt", 2 * k)])
            P.add("dve", lambda e: e.tensor_tensor(out=rt[:, 2 * k + 1, :], in0=ps[b2][:], in1=cst[:, 1, :], op=ALU.mult),
                  reads=[bk(b2), "cst"], writes=[("rt", 2 * k + 1)])
            P.add(ROPE_ADD_ENG, lambda e: e.tensor_tensor(out=dst, in0=rt[:, 2 * k, :], in1=rt[:, 2 * k + 1, :], op=ALU.add),
                  reads=[("rt", 2 * k), ("rt", 2 * k + 1)], writes=dkeys)

        def rope_flush():
            prev = list(rope_pending)
            del rope_pending[:]
            for it in prev:
                rope_finish(*it)

        rope_pending = []
        rope_ctr = [0]

        def proj_v(wt, wkey, ncols, col0, n, dsts):
            wv = wt[:, 0:8 * ncols].rearrange("p (c n) -> p c n", n=ncols)
            for tb in range(4):
                b = next_bank()
                for c in range(8):
                    P.add("pe", lambda e, c=c, tb=tb, b=b: e.matmul(ps[b][:, 0:n], lhsT=xn[:, c, tb * 128:(tb + 1) * 128], rhs=wv[:, c, col0:col0 + n],
                                                                    start=(c == 0), stop=(c == 7)),
                          reads=[wkey, ("xn", c)], writes=[bk(b)])
                for (dst, dkeys, lo, hi) in dsts(tb):
                    eng = "act" if tb % 2 == 0 else "dve"
                    if eng == "act":
                        P.add("act", lambda e, dst=dst, b=b, lo=lo, hi=hi: e.copy(out=dst, in_=ps[b][:, lo:hi]), reads=[bk(b)], writes=dkeys)
                    else:
                        P.add("dve", lambda e, dst=dst, b=b, lo=lo, hi=hi: e.tensor_copy(out=dst, in_=ps[b][:, lo:hi]), reads=[bk(b)], writes=dkeys)

        def kslot(g, i):
            return i % NR[g]

        def kv_proj(i, groups):
            if any(g in groups for g in (0, 1)):
                wt, wkey = next_piece("ak0")
                for hd_ in range(4):
                    g, hh = hd_ // 2, hd_ % 2
                    proj_fm(wt, wkey, 512, hd_ * 128, True, lambda g=g, hh=hh: (Kr[g][:, kslot(g, i), hh, :], [("K", g, kslot(g, i), hh)]))
            wt, wkey = next_piece("ak1")
            for hh in range(2):
                proj_fm(wt, wkey, 256, hh * 128, True, lambda hh=hh: (Kr[2][:, kslot(2, i), hh, :], [("K", 2, kslot(2, i), hh)]))
            rope_flush()
            if any(g in groups for g in (0, 1)):
                wt, wkey = next_piece("av0")
                proj_v(wt, wkey, 512, 0, 512,
                       lambda tb: [(Vr[0][:, kslot(0, i), tb, :], [("V", 0, kslot(0, i), tb)], 0, 256),
                                   (Vr[1][:, kslot(1, i), tb, :], [("V", 1, kslot(1, i), tb)], 256, 512)])
            wt, wkey = next_piece("av1")
            proj_v(wt, wkey, 256, 0, 256, lambda tb: [(Vr[2][:, kslot(2, i), tb, :], [("V", 2, kslot(2, i), tb)], 0, 256)])

        def bkv_proj(i, wt, wkey):
            for hh in range(2):
                proj_fm(wt, wkey, 512, hh * 128, True, lambda hh=hh: (Kr[3][:, kslot(3, i), hh, :], [("K", 3, kslot(3, i), hh)]))
            rope_flush()
            proj_v(wt, wkey, 512, 256, 256, lambda tb: [(Vr[3][:, kslot(3, i), tb, :], [("V", 3, kslot(3, i), tb)], 0, 256)])

        def blocks_for(i, g):
            out = []
            if g in (0, 3):
                mo = M_A1 if g == 0 else M_B
                out.append((i - 1, 3, 0, 128, mo + 128))
                for jk in range(4):
                    q0, q1 = 128 * jk, min(512, 128 * (jk + 2))
                    out.append((i, jk, q0, q1, mo))
            elif g == 1:
                for jk in range(4):
                    out.append((i - 1, jk, 0, 128 * (jk + 1), M_G2PREV + 128 * (3 - jk)))
                for jk in range(4):
                    out.append((i, jk, 128 * jk, 512, M_G2OWN))
            else:
                for it in (i - 3, i - 2, i - 1):
                    for jk in range(4):
                        out.append((it, jk, 0, 512, M_G3ALL))
                for jk in range(4):
                    out.append((i - 4, jk, 0, 128 * (jk + 1), M_G3PREV + 128 * (3 - jk)))
                for jk in range(4):
                    out.append((i, jk, 128 * jk, 512, M_G3OWN))
            return out

        pb_ctr = [0]

        def attend(work, nb, db, finish):
            P.add("pe", lambda e: e.matmul(ps[nb][:], lhsT=zeros_b[:], rhs=xn[:, 0, :], start=True, stop=False),
                  reads=["zeros_b", ("xn", 0)], writes=[bk(nb)])
            P.add("pe", lambda e: e.matmul(ps[db][:], lhsT=zeros_b[:], rhs=xn[:, 0, :], start=True, stop=False),
                  reads=["zeros_b", ("xn", 0)], writes=[bk(db)])
            staged = []

            def stage_s(w):
                n = w["q1"] - w["q0"]
                sbank = next_bank(pool=(0, 1, 2, 3))
                k = pb_ctr[0] % 4
                pb_ctr[0] += 1
                has_mask = w.get("mask") is not None
                P.add("pe", lambda e: e.matmul(ps[sbank][:, 0:n], lhsT=w["kT"], rhs=w["qap"][:, w["q0"]:w["q1"]], start=True, stop=not has_mask),
                      reads=w["kkeys"] + [w["qkey"]], writes=[bk(sbank)])
                if has_mask:
                    mo = w["mask"]
                    P.add("pe", lambda e: e.matmul(ps[sbank][:, 0:n], lhsT=rperm[:, 128:256], rhs=masks[:, mo:mo + n], start=False, stop=True),
                          reads=["rperm", "masks"], writes=[bk(sbank)])
                if w["halo"]:
                    P.add("act", lambda e: e.activation(out=Pb[:, k, 0:n], in_=ps[sbank][:, 0:n], func=AF.Exp, bias=hb[:, 0:1], scale=SCALE),
                          reads=[bk(sbank), "hb"], writes=[("Pb", k)])
                else:
                    P.add("act", lambda e: e.activation(out=Pb[:, k, 0:n], in_=ps[sbank][:, 0:n], func=AF.Exp, scale=SCALE),
                          reads=[bk(sbank)], writes=[("Pb", k)])
                staged.append((w, k, n))

            def stage_pv(last):
                w, k, n = staged.pop(0)
                P.add("pe", lambda e: e.matmul(ps[nb][:, w["q0"]:w["q1"]], lhsT=w["vT"], rhs=Pb[:, k, 0:n], start=False, stop=last),
                      reads=w["vkeys"] + [("Pb", k)], writes=[bk(nb)])
                P.add("pe", lambda e: e.matmul(ps[db][:, w["q0"]:w["q1"]], lhsT=ones_b[:], rhs=Pb[:, k, 0:n], start=False, stop=last),
                      reads=["ones_b", ("Pb", k)], writes=[bk(db)])

            LOOK = 3
            for idx, w in enumerate(work):
                stage_s(w)
                if idx >= LOOK:
                    stage_pv(False)
            while staged:
                stage_pv(len(staged) == 1)
            finish(nb, db)

        def finish_plain(oc):
            def fin(nb, db):
                P.add("dve", lambda e: e.reciprocal(out=rec, in_=ps[db][:]), reads=[bk(db)], writes=["rec"])
                P.add("dve", lambda e: e.tensor_tensor(out=oT[:, oc, :], in0=ps[nb][:], in1=rec, op=ALU.mult),
                      reads=[bk(nb), "rec"], writes=[("oT", oc)])
            return fin

        def finish_sink(oc, h):
            def fin(nb, db):
                P.add("dve", lambda e: e.tensor_scalar(out=rec, in0=ps[db][:], scalar1=esink[:, h:h + 1], scalar2=None, op0=ALU.add),
                      reads=[bk(db), "esink"], writes=["rec"])
                P.add("dve", lambda e: e.reciprocal(out=rec, in_=rec), reads=["rec"], writes=["rec"])
                P.add("dve", lambda e: e.tensor_tensor(out=oT[:, oc, :], in0=ps[nb][:], in1=rec, op=ALU.mult),
                      reads=[bk(nb), "rec"], writes=[("oT", oc)])
            return fin

        nd_ctr = [0]

        def nd_banks():
            k = nd_ctr[0] % 2
            nd_ctr[0] += 1
            return (4, 5) if k == 0 else (6, 7)

        def mixer(i):
            hch = [hbuf[:, c, :] for c in range(8)]
            hkeys = [("h", c) for c in range(8)]
            P.seed(MIX_KEYS, FFN_KEYS)
            rms_stats(hch, hkeys)
            normalize(hch, hkeys, 2)
            wt, wkey = next_piece("aq0")
            for hd_ in range(4):
                proj_fm(wt, wkey, 512, hd_ * 128, True, lambda hd_=hd_: (qT[:, hd_, :], ["qT%d" % hd_]))
            wt, wkey = next_piece("aq1")
            for hd_ in range(2):
                proj_fm(wt, wkey, 256, hd_ * 128, True, lambda hd_=hd_: (qT[:, 4 + hd_, :], ["qT%d" % (4 + hd_)]))
            kv_proj(i, (0, 1, 2))
            if stage == 21:
                return
            for hh in range(2):
                work = []
                for g in (2, 1, 0):
                    for (it, jk, q0, q1, mo) in blocks_for(i, g):
                        s = kslot(g, it)
                        work.append(dict(kT=Kr[g][:, s, hh, jk * 128:(jk + 1) * 128], kkeys=[("K", g, s, hh)],
                                         vT=Vr[g][:, s, jk, hh * 128:(hh + 1) * 128], vkeys=[("V", g, s, jk)],
                                         qap=qT[:, 2 * g + hh, :], qkey="qT%d" % (2 * g + hh), q0=q0, q1=q1, mask=mo, halo=(it < NHALO)))
                nb, db = nd_banks()
                attend(work, nb, db, finish_plain(hh))
            if stage == 22:
                return
            wt, wkey = next_piece("bq")
            for hd_ in range(4):
                proj_fm(wt, wkey, 512, hd_ * 128, True, lambda hd_=hd_: (qT[:, hd_, :], ["qT%d" % hd_]))
            wt, wkey = next_piece("bkv")
            bkv_proj(i, wt, wkey)
            for h in range(4):
                kvh = h // 2
                work = []
                for (it, jk, q0, q1, mo) in blocks_for(i, 3):
                    s = kslot(3, it)
                    work.append(dict(kT=Kr[3][:, s, kvh, jk * 128:(jk + 1) * 128], kkeys=[("K", 3, s, kvh)],
                                     vT=Vr[3][:, s, jk, kvh * 128:(kvh + 1) * 128], vkeys=[("V", 3, s, jk)],
                                     qap=qT[:, h, :], qkey="qT%d" % h, q0=q0, q1=q1, mask=mo, halo=(it < NHALO)))
                nb, db = nd_banks()
                attend(work, nb, db, finish_sink(2 + h, h))
            if stage == 23:
                return
            wt, wkey = next_piece("mq")
            for hd_ in range(4):
                proj_fm(wt, wkey, 512, hd_ * 128, False, lambda hd_=hd_: (qT[:, hd_, :], ["qT%d" % hd_]))
            for h in range(4):
                work = []
                for b_ in range(2):
                    work.append(dict(kT=mkT[:, h, b_ * 128:(b_ + 1) * 128], kkeys=["mkT"],
                                     vT=mv[:, b_, h * 128:(h + 1) * 128], vkeys=["mv"],
                                     qap=qT[:, h, :], qkey="qT%d" % h, q0=0, q1=512, mask=None, halo=False))
                nb, db = nd_banks()
                attend(work, nb, db, finish_plain(6 + h))
            if stage == 24:
                return
            P.seed([("merged", k) for k in range(8)], ["qT%d" % k for k in range(6)] + [("xb", 0), ("xb", 1)])
            order = [(2, 0), (4, 2), (4, 6)]
            for br, (nk, oc0) in enumerate(order):
                for gi_ in range(2):
                    wt_g, wkey_g = next_piece("mg%d%d" % (br, gi_))
                    wg_v = wt_g[:, 0:4096].rearrange("p (c n) -> p c n", n=512)
                    wo_v = wt_g[:, 4096:4096 + nk * 512].rearrange("p (c n) -> p c n", n=512)
                    for o4 in range(4):
                        o = gi_ * 4 + o4
                        bp, bgz = next_bank(), next_bank()
                        for kc in range(nk):
                            P.add("pe", lambda e, kc=kc, bp=bp, o4=o4, wo_v=wo_v, oc0=oc0, nk=nk: e.matmul(ps[bp][:], lhsT=wo_v[:, kc, o4 * 128:(o4 + 1) * 128], rhs=oT[:, oc0 + kc, :],
                                                                              start=(kc == 0), stop=(kc == nk - 1)),
                                  reads=[wkey_g, ("oT", oc0 + kc)], writes=[bk(bp)])
                        for c in range(8):
                            P.add("pe", lambda e, c=c, bgz=bgz, o4=o4, wg_v=wg_v: e.matmul(ps[bgz][:], lhsT=wg_v[:, c, o4 * 128:(o4 + 1) * 128], rhs=xn[:, c, :],
                                                                                start=(c == 0), stop=(c == 7)),
                                  reads=[wkey_g, ("xn", c)], writes=[bk(bgz)])
                        sk = sig_ctr[0] % 2
                        sig_ctr[0] += 1
                        gcol = br * 8 + o
                        P.add("act", lambda e, sk=sk, bgz=bgz, gcol=gcol: e.activation(out=sig[:, sk, :], in_=ps[bgz][:], func=AF.Sigmoid,
                                                                                      bias=bgate[:, gcol:gcol + 1], scale=1.0),
                              reads=[bk(bgz), "bgate"], writes=[("sig", sk)])
                        if br == 0:
                            P.add("dve", lambda e, sk=sk, bp=bp, o=o: e.tensor_tensor(out=fbuf[:, o, :], in0=sig[:, sk, :], in1=ps[bp][:], op=ALU.mult),
                                  reads=[("sig", sk), bk(bp)], writes=[("f", o)])
                        else:
                            P.add("dve", lambda e, sk=sk, bp=bp: e.tensor_tensor(out=prod, in0=sig[:, sk, :], in1=ps[bp][:], op=ALU.mult),
                                  reads=[("sig", sk), bk(bp)], writes=["prod"])
                            if br == 1:
                                P.add("dve", lambda e, o=o: e.tensor_tensor(out=fbuf[:, o, :], in0=fbuf[:, o, :], in1=prod, op=ALU.add),
                                      reads=[("f", o), "prod"], writes=[("f", o)])
                            else:
                                P.add("dve", lambda e, o=o: e.tensor_tensor(out=merged[:, o, :], in0=fbuf[:, o, :], in1=prod, op=ALU.add),
                                      reads=[("f", o), "prod"], writes=[("merged", o)])
            for m in range(2):
                wt, wkey = next_piece("wo%d" % m)
                wv = wt[:, 0:4096].rearrange("p (c n) -> p c n", n=512)
                for o4 in range(4):
                    o = m * 4 + o4
                    b = next_bank()
                    for c in range(8):
                        P.add("pe", lambda e, c=c, b=b, o4=o4, wv=wv: e.matmul(ps[b][:], lhsT=wv[:, c, o4 * 128:(o4 + 1) * 128], rhs=merged[:, c, :],
                                                                        start=(c == 0), stop=(c == 7)),
                              reads=[wkey, ("merged", c)], writes=[bk(b)])
                    stats_square(o, ps[b][:], [bk(b)])
                    P.add("dve", lambda e, o=o, b=b: e.tensor_copy(out=fbuf[:, o, :], in_=ps[b][:]), reads=[bk(b), ("sq8", o)], writes=[("f", o)])
            stats_finish()
            for c in range(8):
                P.add("dve", lambda e, c=c: e.scalar_tensor_tensor(out=fbuf[:, c, :], in0=fbuf[:, c, :], scalar=gains[:, 3, c:c + 1],
                                                                   in1=rstd[:], op0=ALU.mult, op1=ALU.mult),
                      reads=[("f", c), "gains", "rstd"], writes=[("f", c)])
                P.add("dve", lambda e, c=c: e.tensor_tensor(out=hbuf[:, c, :], in0=fbuf[:, c, :], in1=hbuf[:, c, :], op=ALU.add),
                      reads=[("f", c), ("h", c)], writes=[("h", c)])

        sig_ctr = [0]

        load_x(tiles[0])
        load_cs(tiles[0])
        for k2, nm in enumerate(("mem0", "mem1")):
            c0 = 512 * k2
            o_ = wslot[1 + k2][:, 0:4096].rearrange("p (c n) -> p c n", n=512)
            s_ = dr["w_mem_kv"][:, c0:c0 + 512].rearrange("(c p) n -> p c n", p=128)
            P.add("pool", lambda e, o_=o_, s_=s_: e.dma_start(out=o_, in_=s_), writes=[("wslot", 1 + k2)], dma_key=("memw", k2))
        P.add("sp", lambda e: e.dma_start(out=fbuf[:, :, 0:256], in_=memT.rearrange("(c p) t -> p c t", p=128)),
              writes=[("f", c) for c in range(8)], dma_key="memload")
        GSZ = 3
        for k_, nm in enumerate(used_pieces):
            piece_group[nm] = ("wscg", k_ // GSZ)
            cast_piece(nm, wsc[pid[nm]], ("wsc", nm), piece_group[nm])
        mch = [fbuf[:, c, 0:256] for c in range(8)]
        mkeys = [("f", c) for c in range(8)]
        rms_stats(mch, mkeys, ncols=256)
        normalize(mch, mkeys, 6, ncols=256)
        wk_v = wslot[1][:, 0:4096].rearrange("p (c n) -> p c n", n=512)
        wv_v = wslot[2][:, 0:4096].rearrange("p (c n) -> p c n", n=512)
        for h in range(4):
            b = next_bank()
            for c in range(8):
                P.add("pe", lambda e, c=c, b=b, h=h: e.matmul(ps[b][:, 0:256], lhsT=wk_v[:, c, h * 128:(h + 1) * 128], rhs=xn[:, c, 0:256],
                                                              start=(c == 0), stop=(c == 7)),
                      reads=[("wslot", 1), ("xn", c)], writes=[bk(b)])
            P.add("act", lambda e, b=b, h=h: e.copy(out=mkT[:, h, :], in_=ps[b][:, 0:256]), reads=[bk(b)], writes=["mkT"])
        for tb in range(2):
            b = next_bank()
            for c in range(8):
                P.add("pe", lambda e, c=c, b=b, tb=tb: e.matmul(ps[b][:], lhsT=xn[:, c, tb * 128:(tb + 1) * 128], rhs=wv_v[:, c, :],
                                                                start=(c == 0), stop=(c == 7)),
                      reads=[("wslot", 2), ("xn", c)], writes=[bk(b)])
            P.add("dve", lambda e, b=b, tb=tb: e.tensor_copy(out=mv[:, tb, :], in_=ps[b][:]), reads=[bk(b)], writes=["mv"])

        xin_keys = ["xin"] * 8
        for ti, i in enumerate(tiles):
            ffn(1, xin, xin_keys, xin, xin_keys)
            nxt = tiles[ti + 1] if ti + 1 < len(tiles) else None
            if i < NHALO:
                hch = [hbuf[:, c, :] for c in range(8)]
                hkeys = [("h", c) for c in range(8)]
                P.seed(MIX_KEYS, FFN_KEYS)
                rms_stats(hch, hkeys)
                normalize(hch, hkeys, 2)
                if i == NHALO - 1:
                    kv_proj(i, (0, 1, 2))
                    wt, wkey = next_piece("bkv")
                    bkv_proj(i, wt, wkey)
                else:
                    kv_proj(i, (2,))
                if nxt is not None:
                    load_x(nxt)
                    load_cs(nxt)
                continue
            if stage == 1:
                store_y(i)
                if nxt is not None:
                    load_x(nxt)
                continue
            mixer(i)
            if nxt is not None:
                load_cs(nxt)
            if stage >= 2:
                store_y(i)
                if nxt is not None:
                    load_x(nxt)
                continue
            hb_ = hbuf
            hk_ = [("h", c) for c in range(8)]
            if nxt is not None:
                load_x(nxt)
            ffn(2, hb_, hk_, hb_, hk_)
            store_y(i)
        P.add("sp", None, reads=[("y", i) for i in tiles if i >= NHALO])
        P.emit_all(nc)
    return nc, P


def host_constants():
    k = np.arange(128)[:, None]
    q = np.arange(128)[None, :]
    GE = (q >= k)
    LE = (q <= k)
    LT = (q < k)
    M4 = ((q - k) % 4 == 0)
    M16 = ((q - k) % 16 == 0)
    m = np.zeros((128, NMASK), np.float32)
    m[:, M_A1:M_A1 + 256] = np.concatenate([GE, LE], 1)
    m[:, M_B:M_B + 256] = np.concatenate([GE, LT], 1)
    m[:, M_G2OWN:M_G2OWN + 512] = np.concatenate([M4 & GE, M4, M4, M4], 1)
    m[:, M_G2PREV:M_G2PREV + 512] = np.concatenate([M4, M4, M4, M4 & LE], 1)
    m[:, M_G3OWN:M_G3OWN + 512] = np.concatenate([M16 & GE, M16, M16, M16], 1)
    m[:, M_G3ALL:M_G3ALL + 512] = np.concatenate([M16, M16, M16, M16], 1)
    m[:, M_G3PREV:M_G3PREV + 512] = np.concatenate([M16, M16, M16, M16 & LE], 1)
    m = (m - 1.0) * 30000.0
    rp = np.zeros((128, 256), np.float32)
    for mm in range(128):
        rp[(mm + 64) % 128, mm] = 1.0
        rp[mm, 128 + mm] = 1.0
    return m.astype(np.float32), rp


def chunked(v):
    return np.ascontiguousarray(np.asarray(v, np.float32).reshape(8, 128).T)


_CACHE = {}


def make_in_maps(inputs):
    x = np.asarray(inputs["x"], np.float32)
    mem = np.asarray(inputs["mem"], np.float32)
    masks, rp = host_constants()
    gains = np.stack([chunked(inputs[n][0]) for n in ("ffn1_norm_pre", "ffn1_norm_post", "mix_norm_pre", "mix_norm_post",
                                                      "ffn2_norm_pre", "ffn2_norm_post", "mem_norm")], axis=1)
    gains = np.ascontiguousarray(gains, np.float32)
    bg = np.ascontiguousarray(np.asarray(inputs["b_gate"][0], np.float32).reshape(24, 128).T)
    sinks = np.ascontiguousarray(np.asarray(inputs["sinks"], np.float32).reshape(1, 4))
    wmap = {n: np.ascontiguousarray(np.asarray(inputs[n][0], np.float32)) for n in WNAMES}
    inv = (np.float32(10000.0) ** (-(np.arange(64, dtype=np.float32) / np.float32(64)))).astype(np.float32)
    in_maps = []
    for c in range(8):
        b, a = c // 4, c % 4
        t0 = a * NOWN * TT
        lo = t0 - NHALO * TT
        xt = np.zeros((D, NTOK), np.float32)
        if lo >= 0:
            xt[:, :] = x[b, lo:t0 + NOWN * TT, :].T
        else:
            xt[:, NHALO * TT:] = x[b, 0:NOWN * TT, :].T
        pos = np.arange(lo, t0 + NOWN * TT).astype(np.float32)
        ang = pos[:, None] * inv[None, :]
        cos = np.cos(ang).astype(np.float32).T
        sin = np.sin(ang).astype(np.float32).T
        cs = np.empty((128, 2, NTOK), np.float32)
        cs[0:64, 0] = cos
        cs[64:128, 0] = cos
        cs[0:64, 1] = -sin
        cs[64:128, 1] = sin
        hbv = np.full((128, 1), 0.0 if a > 0 else -30000.0, np.float32)
        m = dict(wmap)
        m.update(xT=xt, cs=cs, hb=hbv, memT=np.ascontiguousarray(mem[b].T), masks=masks, rperm=rp, gains=gains,
                 bgate=bg, sinks=sinks)
        in_maps.append(m)
    return in_maps


def kernel(**inputs):
    if "nc" not in _CACHE:
        _CACHE["nc"] = build_program()[0]
    nc = _CACHE["nc"]
    in_maps = make_in_maps(inputs)
    res = run_bass_kernel_spmd(nc, in_maps, core_ids=list(range(8)))
    out = np.empty((2, 16384, D), np.float32)
    for c in range(8):
        b, a = c // 4, c % 4
        t0 = a * NOWN * TT
        out[b, t0:t0 + NOWN * TT, :] = res.results[c]["yT"].T
    return out
```
